# Optimizing a Trainium2 kernel written in Bass

```python
import math
import jax, jax.numpy as jnp
from jax import lax
import numpy as np

D_MODEL = 1024
BATCH = 16
SEQ = 2048
DEPTH = 4

N_Q_HEADS = 8
N_KV_HEADS = 2
HEAD_DIM = 64
ATTN_WIDTH = N_Q_HEADS * HEAD_DIM
KV_WIDTH = N_KV_HEADS * HEAD_DIM
WINDOW = 128
ATTN_BLOCK = 128
GMLP_GROUPS = 4
GMLP_GROUP_DIM = 128
GMLP_WIDTH = GMLP_GROUPS * GMLP_GROUP_DIM
CHUNK = 128
SPLITS = (ATTN_WIDTH, KV_WIDTH, KV_WIDTH, ATTN_WIDTH,
          GMLP_WIDTH, GMLP_WIDTH, GMLP_WIDTH, D_MODEL, D_MODEL)
IN_WIDTH = sum(SPLITS)
ALPHA = (2 * DEPTH) ** 0.25
BETA = (8 * DEPTH) ** -0.25
LN_EPS = 1e-5

kernel_name = "hybrid_swa_sink_gmlp_deepnorm"


def layer_norm(x, g, b):
    xf = x.astype(jnp.float32)
    mu = xf.mean(-1, keepdims=True)
    var = jnp.square(xf - mu).mean(-1, keepdims=True)
    y = (xf - mu) * lax.rsqrt(var + LN_EPS)
    return (y * g.astype(jnp.float32) + b.astype(jnp.float32)).astype(x.dtype)


def alibi_slopes():
    h = jnp.arange(N_Q_HEADS, dtype=jnp.float32)
    return jnp.exp2(-8.0 * (h + 1.0) / N_Q_HEADS)


def split_columns(h):
    offs = [int(o) for o in np.cumsum(SPLITS)[:-1]]
    return jnp.split(h, offs, axis=-1)


def sliding_window_attention(q, k, v, sinks):
    B, S = q.shape[0], q.shape[1]
    nb = S // ATTN_BLOCK
    grp = N_Q_HEADS // N_KV_HEADS
    qb = q.reshape(B, nb, ATTN_BLOCK, N_KV_HEADS, grp, HEAD_DIM)

    def band(t):
        tb = t.reshape(B, nb, ATTN_BLOCK, N_KV_HEADS, HEAD_DIM)
        prev = jnp.pad(tb, ((0, 0), (1, 0), (0, 0), (0, 0), (0, 0)))[:, :-1]
        return jnp.concatenate([prev, tb], axis=2)

    kb, vb = band(k), band(v)
    scores = jnp.einsum('bnqhgd,bnkhd->bnhgqk', qb, kb).astype(jnp.float32) * (HEAD_DIM ** -0.5)
    qi = jnp.arange(ATTN_BLOCK)[:, None]
    kj = jnp.arange(2 * ATTN_BLOCK)[None, :]
    dist = qi + ATTN_BLOCK - kj
    blk = jnp.arange(nb)[:, None, None]
    key_pos = blk * ATTN_BLOCK + qi - dist
    valid = (dist >= 0) & (dist < WINDOW) & (key_pos >= 0)
    slopes = alibi_slopes().reshape(N_KV_HEADS, grp)
    scores = scores - slopes[:, :, None, None] * dist.astype(jnp.float32)
    scores = jnp.where(valid[None, :, None, None], scores, -jnp.inf)
    sink = sinks.astype(jnp.float32).reshape(N_KV_HEADS, grp)[None, None, :, :, None, None]
    m = jnp.maximum(scores.max(-1, keepdims=True), sink)
    p = jnp.exp(scores - m)
    denom = p.sum(-1, keepdims=True) + jnp.exp(sink - m)
    out = jnp.einsum('bnhgqk,bnkhd->bnqhgd', (p / denom).astype(v.dtype), vb)
    return out.reshape(B, S, ATTN_WIDTH)


def chunked_spatial_gating(u, v, ln_g, ln_b, w_s, b_s):
    B, S = v.shape[0], v.shape[1]
    nc = S // CHUNK
    vn = layer_norm(v, ln_g, ln_b).reshape(B, nc, CHUNK, GMLP_GROUPS, GMLP_GROUP_DIM)
    causal = jnp.tril(jnp.ones((CHUNK, CHUNK), dtype=bool))
    w = jnp.where(causal[None], w_s, jnp.zeros_like(w_s))
    mixed = jnp.einsum('gts,bcsgd->bctgd', w, vn) + b_s.T[None, None, :, :, None]
    return u * mixed.reshape(B, S, GMLP_WIDTH)


def setup_inputs(seed: int = 0) -> dict:
    key = jax.random.key(seed)
    ks = jax.random.split(key, 16)
    f32 = jnp.float32
    x = jax.random.normal(ks[0], (BATCH, SEQ, D_MODEL), f32)
    w_in = jax.random.normal(ks[1], (DEPTH, D_MODEL, IN_WIDTH), f32) * D_MODEL ** -0.5
    b_in = jax.random.normal(ks[2], (DEPTH, IN_WIDTH), f32) * 0.02
    attn_sinks = jax.random.normal(ks[3], (DEPTH, N_Q_HEADS), f32) * 0.5
    gmlp_ln_g = 1.0 + 0.02 * jax.random.normal(ks[4], (DEPTH, GMLP_WIDTH), f32)
    gmlp_ln_b = 0.02 * jax.random.normal(ks[5], (DEPTH, GMLP_WIDTH), f32)
    w_spatial = jax.random.normal(ks[6], (DEPTH, GMLP_GROUPS, CHUNK, CHUNK), f32) * (0.5 * CHUNK ** -0.5)
    b_spatial = 1.0 + 0.02 * jax.random.normal(ks[7], (DEPTH, GMLP_GROUPS, CHUNK), f32)
    w_branch_attn = jax.random.normal(ks[8], (DEPTH, ATTN_WIDTH, D_MODEL), f32) * (ATTN_WIDTH ** -0.5 * BETA)
    w_branch_gmlp = jax.random.normal(ks[9], (DEPTH, GMLP_WIDTH, D_MODEL), f32) * (GMLP_WIDTH ** -0.5 * BETA)
    w_out = jax.random.normal(ks[10], (DEPTH, D_MODEL, D_MODEL), f32) * (D_MODEL ** -0.5 * BETA)
    b_out = 0.02 * jax.random.normal(ks[11], (DEPTH, D_MODEL), f32)
    ln_g = 1.0 + 0.02 * jax.random.normal(ks[12], (DEPTH, D_MODEL), f32)
    ln_b = 0.02 * jax.random.normal(ks[13], (DEPTH, D_MODEL), f32)
    return {"x": x, "w_in": w_in, "b_in": b_in, "attn_sinks": attn_sinks,
            "gmlp_ln_g": gmlp_ln_g, "gmlp_ln_b": gmlp_ln_b,
            "w_spatial": w_spatial, "b_spatial": b_spatial,
            "w_branch_attn": w_branch_attn, "w_branch_gmlp": w_branch_gmlp,
            "w_out": w_out, "b_out": b_out, "ln_g": ln_g, "ln_b": ln_b}


def reference(x, w_in, b_in, attn_sinks, gmlp_ln_g, gmlp_ln_b, w_spatial, b_spatial,
              w_branch_attn, w_branch_gmlp, w_out, b_out, ln_g, ln_b):
    B, S = x.shape[0], x.shape[1]
    for l in range(DEPTH):
        h = jnp.einsum('bsd,de->bse', x, w_in[l]) + b_in[l]
        q, k, v, z_a, u_g, v_g, z_g, g_a, g_g = split_columns(h)
        y_a = sliding_window_attention(q.reshape(B, S, N_Q_HEADS, HEAD_DIM),
                                       k.reshape(B, S, N_KV_HEADS, HEAD_DIM),
                                       v.reshape(B, S, N_KV_HEADS, HEAD_DIM),
                                       attn_sinks[l]) * jax.nn.silu(z_a)
        y_g = chunked_spatial_gating(jax.nn.gelu(u_g, approximate=False),
                                     jax.nn.gelu(v_g, approximate=False),
                                     gmlp_ln_g[l], gmlp_ln_b[l],
                                     w_spatial[l], b_spatial[l]) * jax.nn.silu(z_g)
        br_a = jnp.einsum('bse,ed->bsd', y_a, w_branch_attn[l])
        br_g = jnp.einsum('bse,ed->bsd', y_g, w_branch_gmlp[l])
        merged = jax.nn.sigmoid(g_a) * br_a + jax.nn.sigmoid(g_g) * br_g
        out = jnp.einsum('bsd,de->bse', merged, w_out[l]) + b_out[l]
        x = layer_norm(ALPHA * x + out, ln_g[l], ln_b[l])
    return x
```

```python
import contextlib
import numpy as np
import concourse.bass as bass
import concourse.mybir as mybir
from concourse.bass_utils import run_bass_kernel_spmd

F32 = mybir.dt.float32
BF16 = mybir.dt.bfloat16
AF = mybir.ActivationFunctionType
ALU = mybir.AluOpType

D = 1024
KC = 8
NB = 4
SBT = 512
NCHUNK = 14
CH = 4096
NW = 5
ALPHA = 8.0 ** 0.25
LN_EPS = 1e-5
PCW = 40
PROW = 5248
O_BV, O_BVG, O_GLG, O_GLB, O_BS = 0, 128, 640, 1152, 1664
PB1W = 2176
O_BOUT, O_LNG, O_LNB = 0, 1024, 2048
PB2W = 3072
CSTW = 2048 + 128 + 128
FOLD_WAITS = True


class Op:
    __slots__ = ("eng", "tok")

    def __init__(self, eng):
        self.eng = eng
        self.tok = None


class Planner:
    ENGS = ("pe", "act", "dve", "pool", "sp")

    def __init__(self, nc, stack):
        self.nc = nc
        self.stack = stack
        self.streams = {e: [] for e in self.ENGS}
        self.waited = {e: {} for e in self.ENGS}
        self.lastw = {}
        self.readers = {}
        self.esem = {}
        self.ecnt = {}
        for e in ("pe", "act", "dve", "pool"):
            self.esem[e] = stack.enter_context(nc.semaphore("es_" + e))
            self.ecnt[e] = 0
        self.pending = {e: [] for e in ("pe", "act", "dve", "pool")}
        self.dsem = {}
        self.dval = {}
        self.semkey = {}

    def dma_sem(self, name):
        if name not in self.dsem:
            self.dsem[name] = self.stack.enter_context(self.nc.semaphore("ds_" + name))
            self.dval[name] = 0
        return name

    def emit(self, eng, fn, reads=(), writes=(), sig=True, dma=None, nofold=False):
        deps = []
        for k in reads:
            w = self.lastw.get(k)
            if w is not None:
                deps.append(w)
        for k in writes:
            w = self.lastw.get(k)
            if w is not None:
                deps.append(w)
            deps.extend(self.readers.get(k, ()))
        stream = self.streams[eng]
        wd = self.waited[eng]
        ws = []
        for d in deps:
            if d.eng == "pe" and eng == "pe":
                continue
            assert d.tok is not None, "dependency on unsignalled op"
            sname, val = d.tok
            if wd.get(sname, 0) >= val:
                continue
            wd[sname] = val
            ws = [w for w in ws if w[0] != sname]
            ws.append((sname, val))
        ws = [(self._sem(n), v) for n, v in ws]
        fold = None
        if ws and dma is None and FOLD_WAITS and not nofold:
            fold = ws.pop()
        for sem, val in ws:
            stream.append(lambda e, sem=sem, val=val: e.wait_ge(sem, val))
        op = Op(eng)
        if dma is not None:
            self.dval[dma] += 16
            op.tok = (dma, self.dval[dma])
            sem = self.dsem[dma]
            stream.append(lambda e, fn=fn, sem=sem: fn(e).then_inc(sem, 16))
        else:
            isem = None
            if sig:
                self.ecnt[eng] += 1
                op.tok = ("es_" + eng, self.ecnt[eng])
                for p in self.pending[eng]:
                    p.tok = op.tok
                self.pending[eng] = []
                isem = self.esem[eng]
            else:
                self.pending[eng].append(op)

            def run(e, fn=fn, fold=fold, isem=isem):
                inst = fn(e)
                if fold is not None:
                    inst.wait_op(fold[0], fold[1], "sem-ge")
                if isem is not None:
                    inst.then_inc(isem, 1)
            stream.append(run)
        for k in writes:
            self.lastw[k] = op
            self.readers[k] = []
        for k in reads:
            if k not in writes:
                self.readers.setdefault(k, []).append(op)
        return op

    def _sem(self, sname):
        if sname.startswith("es_"):
            return self.esem[sname[3:]]
        return self.dsem[sname]

    def wait_tok(self, eng, tok):
        sname, val = tok
        wd = self.waited[eng]
        if wd.get(sname, 0) >= val:
            return
        wd[sname] = val
        sem = self._sem(sname)
        self.streams[eng].append(lambda e, sem=sem, val=val: e.wait_ge(sem, val))


def build_program(nseq, seq, depth):
    nsb_seq = seq // SBT
    nsb = nseq * nsb_seq
    ntok = nseq * seq
    nc = bass.Bass("TRN2", target_bir_lowering=False)
    x_d = nc.dram_tensor("x", [ntok, D], F32, kind="ExternalInput").ap()
    wpk_d = nc.dram_tensor("wpk", [depth * NCHUNK * 128, CH], F32, kind="ExternalInput").ap()
    prow_d = nc.dram_tensor("prow", [depth, PROW], F32, kind="ExternalInput").ap()
    pcol_d = nc.dram_tensor("pcol", [128, depth * PCW], F32, kind="ExternalInput").ap()
    wsp_d = nc.dram_tensor("wsp", [depth * 128, 512], F32, kind="ExternalInput").ap()
    cst_d = nc.dram_tensor("cst", [128, CSTW], F32, kind="ExternalInput").ap()
    out_d = nc.dram_tensor("out", [ntok, D], F32, kind="ExternalOutput").ap()
    wbf_d = nc.dram_tensor("wbf", [depth * NCHUNK * 128, CH], BF16, kind="Internal").ap()

    with contextlib.ExitStack() as stack:
        def sb(name, shape, dt):
            return stack.enter_context(nc.sbuf_tensor(name, shape, dt))

        def ps(name, shape, dt):
            return stack.enter_context(nc.psum_tensor(name, shape, dt))

        XS = [sb(f"X{i}", [128, NB, D], F32) for i in range(nseq)]
        XB = [sb(f"XB{i}", [128, D], BF16) for i in range(2)]
        XT = sb("XT", [128, KC, SBT], BF16)
        WR = [sb(f"WR{i}", [128, CH], BF16) for i in range(NW)]
        QT = sb("QT", [128, 4, SBT], BF16)
        KT = sb("KT", [128, 5 * 128], BF16)
        KTCS = [sb(f"KTC{i}", [128, depth, 128], BF16) for i in range(nseq)]
        VA = sb("VA", [128, 5, 128], BF16)
        VB = sb("VB", [128, 5, 128], BF16)
        VACS = [sb(f"VAC{i}", [128, depth, 128], BF16) for i in range(nseq)]
        VBCS = [sb(f"VBC{i}", [128, depth, 128], BF16) for i in range(nseq)]
        SZA = sb("SZA", [128, 4, SBT], F32)
        U = sb("U", [128, 4, SBT], F32)
        SZG = [sb(f"SZG{i}", [128, SBT], F32) for i in range(1)]
        VN = sb("VN", [128, NB, 512], BF16)
        YA = sb("YA", [128, 4, SBT], BF16)
        YG = sb("YG", [128, 4, SBT], BF16)
        MG = sb("MG", [128, 8, SBT], BF16)
        SGA = [sb(f"SGA{i}", [128, SBT], F32) for i in range(2)]
        SGG = [sb(f"SGG{i}", [128, SBT], F32) for i in range(2)]
        PT = [sb(f"PT{i}", [128, 2, 2, SBT], BF16) for i in range(2)]
        BIAS8 = sb("BIAS8", [128, 2, 8, 128], F32)
        PB1 = sb("PB1", [128, PB1W], F32)
        PB2 = sb("PB2", [128, PB2W], F32)
        PC = sb("PC", [128, depth, PCW], F32)
        SE = sb("SE", [128, depth, 4], F32)
        WST = sb("WST", [128, depth, 4, 128], BF16)
        WMASK = sb("WMASK", [128, 128], F32)
        IDF = sb("IDF", [128, 128], F32)
        IDENT = sb("IDENT", [128, 128], BF16)
        ONESA = sb("ONESA", [128, 128], BF16)
        ONESB = sb("ONESB", [128, 128], BF16)
        DENS = [sb(f"DENS{i}", [128, 4, 128], F32) for i in range(1)]
        RR = [sb(f"RR{i}", [128, 4, 128], F32) for i in range(1)]
        T1 = [sb(f"T1{i}", [128, 4, 128], F32) for i in range(1)]
        M1 = [sb(f"M1{i}", [128, SBT], F32) for i in range(1)]
        M2 = [sb(f"M2{i}", [128, SBT], F32) for i in range(1)]
        VGF = [sb(f"VGF{i}", [128, 512], F32) for i in range(3)]
        RT = [sb(f"RT{i}", [128, 512], F32) for i in range(2)]
        ST6 = [sb(f"ST6{i}", [128, 12], F32) for i in range(2)]
        MV = [sb(f"MV{i}", [128, 2], F32) for i in range(2)]
        SD = [sb(f"SD{i}", [128, 1], F32) for i in range(2)]
        RS = [sb(f"RS{i}", [128, 1], F32) for i in range(2)]
        NM = [sb(f"NM{i}", [128, 1], F32) for i in range(2)]

        NPS = 8
        PSB = [ps(f"PS{i}", [128, 512], F32) for i in range(NPS)]

        print('sbuf bytes remaining', nc.sbuf_bytes_remaining)
        P = Planner(nc, stack)
        rot = {"ps": 0}

        NGEN = NPS - 2

        def next_bank():
            i = rot["ps"]
            rot["ps"] = (i + 1) % NGEN
            return i

        def rotv(name):
            v = rot.get(name, 0)
            rot[name] = v ^ 1
            return v

        for i in range(4):
            P.dma_sem(f"cst{i}")
        P.emit("sp", lambda e: e.dma_start(out=BIAS8[:].rearrange("p a b c -> p (a b c)"), in_=cst_d[:, 0:2048]),
               writes=["BIAS8"], dma="cst0")
        P.emit("sp", lambda e: e.dma_start(out=WMASK[:], in_=cst_d[:, 2048:2176]), writes=["WMASK"], dma="cst1")
        P.emit("sp", lambda e: e.dma_start(out=IDF[:], in_=cst_d[:, 2176:2304]), writes=["IDF"], dma="cst2")
        P.emit("sp", lambda e: e.dma_start(out=PC[:].rearrange("p a b -> p (a b)"), in_=pcol_d[:, :]),
               writes=["PC"], dma="cst3")
        conv_tok = {}

        def emit_conv(l):
            for c in range(NCHUNK):
                name = P.dma_sem(f"cv{l}_{c}" if l == 0 else f"cv{l}")
                r0 = (l * NCHUNK + c) * 128
                op = P.emit("pool", lambda e, r0=r0: e.dma_start(out=wbf_d[r0:r0 + 128, :], in_=wpk_d[r0:r0 + 128, :],
                                                                 max_dma_last_dim=8192),
                            writes=[("wbf", l, c)], dma=name)
                conv_tok[(l, c)] = op
            if l > 0:
                last = conv_tok[(l, NCHUNK - 1)].tok
                for c in range(NCHUNK):
                    conv_tok[(l, c)].tok = last

        emit_conv(0)

        P.emit("dve", lambda e: e.tensor_copy(out=IDENT[:], in_=IDF[:]), reads=["IDF"], writes=["IDENT"])
        P.emit("dve", lambda e: e.memset(ONESA[:, 0:64], 1.0), writes=["ONESA"])
        P.emit("dve", lambda e: e.memset(ONESA[:, 64:128], 0.0), writes=["ONESA"])
        P.emit("dve", lambda e: e.memset(ONESB[:, 0:64], 0.0), writes=["ONESB"])
        P.emit("dve", lambda e: e.memset(ONESB[:, 64:128], 1.0), writes=["ONESB"])
        P.emit("dve", lambda e: e.memset(VA[:].rearrange("p a b -> p (a b)"), 0.0), writes=["VA"])
        P.emit("dve", lambda e: e.memset(VB[:].rearrange("p a b -> p (a b)"), 0.0), writes=["VB"])
        for i in range(nseq):
            P.emit("dve", lambda e, i=i: e.memset(VACS[i][:].rearrange("p a b -> p (a b)"), 0.0), writes=[("VAC", i)])
            P.emit("dve", lambda e, i=i: e.memset(VBCS[i][:].rearrange("p a b -> p (a b)"), 0.0), writes=[("VBC", i)])
            P.emit("dve", lambda e, i=i: e.memset(KTCS[i][:].rearrange("p a b -> p (a b)"), 0.0), writes=[("KTC", i)])
        P.emit("act", lambda e: e.activation(out=SE[:], in_=PC[:, :, 33:37], func=AF.Exp), reads=["PC"], writes=["SE"])
        P.dma_sem("wsf")

        def emit_wsp():
            for l in range(depth):
                P.emit("sp", lambda e, l=l: e.dma_start(out=VGF[0][:], in_=wsp_d[l * 128:(l + 1) * 128, :]),
                       writes=[("VGF", 0)], dma="wsf")
                for g in range(4):
                    P.emit("dve", lambda e, l=l, g=g: e.tensor_tensor(
                        out=WST[:, l, g, :], in0=VGF[0][:, g * 128:(g + 1) * 128], in1=WMASK[:], op=ALU.mult),
                           reads=[("VGF", 0), "WMASK"], writes=["WST"])

        chunk_seq = [(0, l, c) for p_ in range(nsb_seq) for l in range(depth) for s_ in range(nseq) for c in range(NCHUNK)]
        wstate = {"next": 0, "free": list(range(NW)), "slot": {}}
        for i in range(NW):
            P.dma_sem(f"wr{i}")

        def prefetch():
            while wstate["free"] and wstate["next"] < len(chunk_seq):
                n = wstate["next"]
                sbi, l, c = chunk_seq[n]
                if (l, c) not in conv_tok:
                    break
                slot = wstate["free"].pop(0)
                wstate["slot"][n] = slot
                wstate["next"] = n + 1
                P.wait_tok("sp", conv_tok[(l, c)].tok)
                r0 = (l * NCHUNK + c) * 128
                P.emit("sp", lambda e, slot=slot, r0=r0: e.dma_start(out=WR[slot][:], in_=wbf_d[r0:r0 + 128, :]),
                       writes=[("WR", slot)], dma=f"wr{slot}")

        def wslot(it, c):
            n = it * NCHUNK + c
            if n not in wstate["slot"]:
                prefetch()
            assert n in wstate["slot"], f"weight chunk {n} not loaded (ring too small?)"
            return wstate["slot"][n]

        def wrelease(it, c):
            n = it * NCHUNK + c
            wstate["free"].append(wstate["slot"][n])
            prefetch()

        def fm_tile(it, chunk, pos, evac):
            slot = wslot(it, chunk)
            bi = next_bank()
            for kc in range(KC):
                P.emit("pe", lambda e, bi=bi, slot=slot, kc=kc, pos=pos: e.matmul(
                    PSB[bi][:], lhsT=WR[slot][:, kc * 512 + pos * 128: kc * 512 + pos * 128 + 128],
                    rhs=XT[:, kc, :], start=(kc == 0), stop=(kc == KC - 1)),
                       reads=[("WR", slot), "XT"], writes=[("PS", bi)], sig=(kc == KC - 1))
            evac(bi)

        def act_evac(bi, dst_fn, func, l, col, writes):
            P.emit("act", lambda e: e.activation(out=dst_fn(), in_=PSB[bi][:], func=func, bias=PC[:, l, col:col + 1]),
                   reads=[("PS", bi), "PC"], writes=writes)

        P.dma_sem("pb1")
        P.dma_sem("pb2")
        for i in range(nseq):
            for b in range(NB):
                P.dma_sem(f"x{i}_{b}")
                P.dma_sem(f"o{i}_{b}")

        def load_pb1(l):
            P.emit("sp", lambda e: e.dma_start(out=PB1[:], in_=prow_d[l, 0:PB1W].partition_broadcast(128)),
                   writes=["PB1"], dma="pb1")

        def load_pb2(l):
            P.emit("sp", lambda e: e.dma_start(out=PB2[:], in_=prow_d[l, PB1W:PROW].partition_broadcast(128)),
                   writes=["PB2"], dma="pb2")

        def ln_stats(src_aps, par, reads):
            n = len(src_aps)
            for h, ap_fn in enumerate(src_aps):
                P.emit("dve", lambda e, h=h, ap_fn=ap_fn: e.bn_stats(out=ST6[par][:, h * 6:(h + 1) * 6], in_=ap_fn()),
                       reads=reads, writes=[("ST6", par)])
            P.emit("dve", lambda e: e.bn_aggr(out=MV[par][:], in_=ST6[par][:, 0:6 * n]),
                   reads=[("ST6", par)], writes=[("MV", par)])
            P.emit("pool", lambda e: e.tensor_scalar(out=SD[par][:], in0=MV[par][:, 1:2], scalar1=LN_EPS, scalar2=None,
                                                     op0=ALU.add),
                   reads=[("MV", par)], writes=[("SD", par)])
            P.emit("pool", lambda e: e.tensor_tensor(out=RS[par][:], in0=SD[par][:], in1=NEGH[:], op=ALU.pow),
                   reads=[("SD", par), "NEGH"], writes=[("RS", par)])
            P.emit("dve", lambda e: e.scalar_tensor_tensor(out=NM[par][:], in0=MV[par][:, 0:1], scalar=-1.0,
                                                           in1=RS[par][:], op0=ALU.mult, op1=ALU.mult),
                   reads=[("MV", par), ("RS", par)], writes=[("NM", par)])

        NEGH = sb("NEGH", [128, 1], F32)
        P.emit("dve", lambda e: e.memset(NEGH[:], -0.5), writes=["NEGH"])

        store_toks = []

        xl_done = set()
        f_done = set()
        deferred = []
        ST6E = [sb(f"ST6E{i}", [128, 12], F32) for i in range(NB)]

        deferred_b = []

        def flush_deferred():
            while deferred:
                deferred.pop(0)()
            while deferred_b:
                deferred_b.pop(0)()

        def flush_a():
            while deferred:
                deferred.pop(0)()

        def flush_b(n):
            while deferred_b and n > 0:
                deferred_b.pop(0)()
                n -= 1

        def emit_xload(it_n):
            p_n, l_n, s_n = order[it_n]
            if l_n != 0 or it_n in xl_done:
                return
            xl_done.add(it_n)
            Xn = XS[s_n]
            t0n = s_n * seq + p_n * SBT
            for b in range(NB):
                P.emit("sp", lambda e, b=b, t0n=t0n, Xn=Xn: e.dma_start(
                    out=Xn[:, b, :], in_=x_d[t0n + b * 128: t0n + (b + 1) * 128, :]),
                       writes=[("X", s_n, b)], dma=f"x{s_n}_{b}")

        fcast = {}

        def emit_F_cast(it_n, b):
            p_n, l_n, s_n = order[it_n]
            Xn = XS[s_n]
            xp = rotv("xb")
            fcast[(it_n, b)] = xp
            P.emit("act", lambda e, b=b, xp=xp, Xn=Xn: e.activation(out=XB[xp][:], in_=Xn[:, b, :], func=AF.Identity),
                   reads=[("X", s_n, b)], writes=[("XB", xp)])

        def emit_F_tr(it_n, b):
            xp = fcast[(it_n, b)]
            bi = next_bank()
            for kc in range(KC):
                P.emit("pe", lambda e, xp=xp, kc=kc, bi=bi: e.transpose(
                    out=PSB[bi][:].bitcast(BF16)[:, kc * 128:(kc + 1) * 128], in_=XB[xp][:, kc * 128:(kc + 1) * 128],
                    identity=IDENT[:]),
                       reads=[("XB", xp), "IDENT"], writes=[("PS", bi)], sig=(kc == KC - 1))
            P.emit("dve", lambda e, b=b, bi=bi: e.tensor_copy(
                out=XT[:, :, b * 128:(b + 1) * 128],
                in_=PSB[bi][:].bitcast(BF16).rearrange("p (a b) -> p a b", a=KC)),
                   reads=[("PS", bi)], writes=["XT"])

        def emit_F_block(it_n, b):
            emit_F_cast(it_n, b)
            emit_F_tr(it_n, b)

        n_iter = nsb_seq * depth * nseq
        order = [(p_, l_, s_) for p_ in range(nsb_seq) for l_ in range(depth) for s_ in range(nseq)]
        emit_xload(0)
        load_pb1(0)
        load_pb2(0)
        prefetch()
        emit_wsp()
        for it, (p_, l, s_) in enumerate(order):
            X = XS[s_]
            KTC, VAC, VBC = KTCS[s_], VACS[s_], VBCS[s_]
            tok0 = s_ * seq + p_ * SBT
            first_sb = (p_ == 0)
            last_layer = (l == depth - 1)
            nxt_l = order[it + 1][1] if it + 1 < n_iter else None
            if p_ == 0 and s_ == 0 and l + 1 < depth:
                emit_conv(l + 1)
            prefetch()
            emit_xload(it)
            P.emit("pool", lambda e, l=l, KTC=KTC: e.tensor_copy(out=KT[:, 0:128], in_=KTC[:, l, :]),
                   reads=[("KTC", s_)], writes=["KT"])
            P.emit("pool", lambda e, l=l, VAC=VAC: e.tensor_copy(out=VA[:, 0, :], in_=VAC[:, l, :]),
                   reads=[("VAC", s_)], writes=["VA"])
            P.emit("pool", lambda e, l=l, VBC=VBC: e.tensor_copy(out=VB[:, 0, :], in_=VBC[:, l, :]),
                   reads=[("VBC", s_)], writes=["VB"])
            if it not in f_done:
                flush_deferred()
            else:
                flush_a()
            if it not in f_done:
                f_done.add(it)
                for b in range(NB):
                    emit_F_block(it, b)
            for j in range(4):
                fm_tile(it, 0, j, lambda bi, j=j: act_evac(bi, lambda j=j: QT[:, j, :], AF.Identity, l, j, ["QT"]))
            wrelease(it, 0)
            fm_tile(it, 1, 0, lambda bi: act_evac(bi, lambda: KT[:, 128:640], AF.Identity, l, 4, ["KT"]))
            s1 = wslot(it, 1)
            s2 = wslot(it, 2)

            def vg_ln(vp, b):
                sp_ = rotv("stat")
                ln_stats([lambda vp=vp: VGF[vp][:]], sp_, [("VGF", vp)])
                P.emit("act", lambda e, vp=vp, sp_=sp_: e.activation(
                    out=VGF[vp][:], in_=VGF[vp][:], func=AF.Identity, scale=RS[sp_][:], bias=NM[sp_][:]),
                       reads=[("VGF", vp), ("RS", sp_), ("NM", sp_)], writes=[("VGF", vp)])
                P.emit("pool", lambda e, vp=vp: e.tensor_tensor(
                    out=VGF[vp][:], in0=VGF[vp][:], in1=PB1[:, O_GLG:O_GLG + 512], op=ALU.mult),
                       reads=[("VGF", vp), "PB1"], writes=[("VGF", vp)])
                P.emit("pool", lambda e, vp=vp, b=b: e.tensor_tensor(
                    out=VN[:, b, :], in0=VGF[vp][:], in1=PB1[:, O_GLB:O_GLB + 512], op=ALU.add),
                       reads=[("VGF", vp), "PB1"], writes=["VN"])

            pend_ln = []
            for b in range(NB):
                bg = next_bank()
                for kc in range(KC):
                    P.emit("pe", lambda e, bg=bg, kc=kc, b=b, s2=s2: e.matmul(
                        PSB[bg][:], lhsT=XT[:, kc, b * 128:(b + 1) * 128], rhs=WR[s2][:, kc * 512:(kc + 1) * 512],
                        start=(kc == 0), stop=(kc == KC - 1)),
                           reads=["XT", ("WR", s2)], writes=[("PS", bg)], sig=(kc == KC - 1))
                bv = next_bank()
                for kc in range(KC):
                    P.emit("pe", lambda e, bv=bv, kc=kc, b=b, s1=s1: e.matmul(
                        PSB[bv][:, 0:128], lhsT=XT[:, kc, b * 128:(b + 1) * 128],
                        rhs=WR[s1][:, kc * 512 + 128: kc * 512 + 256],
                        start=(kc == 0), stop=(kc == KC - 1)),
                           reads=["XT", ("WR", s1)], writes=[("PS", bv)], sig=(kc == KC - 1))
                P.emit("dve", lambda e, bv=bv, b=b: e.tensor_tensor(
                    out=VA[:, b + 1, 0:64], in0=PSB[bv][:, 0:64], in1=PB1[:, O_BV:O_BV + 64], op=ALU.add),
                       reads=[("PS", bv), "PB1"], writes=["VA"])
                P.emit("dve", lambda e, bv=bv, b=b: e.tensor_tensor(
                    out=VB[:, b + 1, 64:128], in0=PSB[bv][:, 64:128], in1=PB1[:, O_BV + 64:O_BV + 128], op=ALU.add),
                       reads=[("PS", bv), "PB1"], writes=["VB"])
                vp = b % 3
                P.emit("dve", lambda e, bg=bg, vp=vp: e.tensor_tensor(
                    out=VGF[vp][:], in0=PSB[bg][:], in1=PB1[:, O_BVG:O_BVG + 512], op=ALU.add),
                       reads=[("PS", bg), "PB1"], writes=[("VGF", vp)])
                P.emit("act", lambda e, vp=vp: e.activation(out=VGF[vp][:], in_=VGF[vp][:], func=AF.Gelu),
                       reads=[("VGF", vp)], writes=[("VGF", vp)])
                pend_ln.append((vp, b))
                if len(pend_ln) > 2:
                    vg_ln(*pend_ln.pop(0))
            wrelease(it, 2)
            def za_tile(j):
                ch, pos = [(1, 2), (1, 3), (3, 0), (3, 1)][j]
                fm_tile(it, ch, pos, lambda bi, j=j: act_evac(bi, lambda j=j: SZA[:, j, :], AF.Silu, l, 5 + j, ["SZA"]))
                if j == 1:
                    wrelease(it, 1)

            def att_scores(b):
                first_blk = first_sb and b == 0
                kbs = [1] if first_blk else [0, 1]
                for g in range(2):
                    for kb in kbs:
                        bi = next_bank()
                        ks = b + kb
                        P.emit("pe", lambda e, bi=bi, g=g, ks=ks, b=b: e.matmul(
                            PSB[bi][:].rearrange("p (a b) -> p a b", a=4),
                            lhsT=KT[g * 64:(g + 1) * 64, ks * 128:(ks + 1) * 128],
                            rhs=QT[g * 64:(g + 1) * 64, :, b * 128:(b + 1) * 128], start=True, stop=True),
                               reads=["KT", "QT"], writes=[("PS", bi)])
                        P.emit("dve", lambda e, bi=bi, g=g, kb=kb: e.tensor_tensor(
                            out=PSB[bi][:], in0=PSB[bi][:],
                            in1=BIAS8[:, kb, g * 4:(g + 1) * 4, :].rearrange("p a b -> p (a b)"), op=ALU.add),
                               reads=[("PS", bi), "BIAS8"], writes=[("PS", bi)])
                        pp = b % 2
                        P.emit("act", lambda e, bi=bi, g=g, kb=kb, pp=pp: e.activation(
                            out=PT[pp][:, g, kb, :], in_=PSB[bi][:], func=AF.Exp, scale=0.125),
                               reads=[("PS", bi)], writes=[("PT", pp, g)])
                return kbs

            def att_pv(b, kbs):
                pp = b % 2
                by = NPS - 2
                bd = NPS - 1
                terms = [(g, kb) for g in range(2) for kb in kbs]
                nt = len(terms)
                for i, (g, kb) in enumerate(terms):
                    vt = VA if g == 0 else VB
                    P.emit("pe", lambda e, by=by, g=g, kb=kb, vt=vt, b=b, i=i, nt=nt, pp=pp: e.matmul(
                        PSB[by][:], lhsT=vt[:, b + kb, :], rhs=PT[pp][:, g, kb, :],
                        start=(i == 0), stop=(i == nt - 1)),
                           reads=["VA", "VB", ("PT", pp, g)], writes=[("PS", by)], sig=(i == nt - 1))
                for i, (g, kb) in enumerate(terms):
                    on = ONESA if g == 0 else ONESB
                    P.emit("pe", lambda e, bd=bd, g=g, kb=kb, on=on, i=i, nt=nt, pp=pp: e.matmul(
                        PSB[bd][:], lhsT=on[:], rhs=PT[pp][:, g, kb, :],
                        start=(i == 0), stop=(i == nt - 1)),
                           reads=["ONESA", "ONESB", ("PT", pp, g)], writes=[("PS", bd)], sig=(i == nt - 1))
                dp = 0
                for j in range(4):
                    P.emit("act", lambda e, bd=bd, j=j, dp=dp, l=l: e.activation(
                        out=DENS[dp][:, j, :], in_=PSB[bd][:, j * 128:(j + 1) * 128], func=AF.Ln,
                        bias=SE[:, l, j:j + 1]),
                           reads=[("PS", bd), "SE"], writes=[("DENS", dp)])
                P.emit("act", lambda e, dp=dp: e.activation(
                    out=RR[dp][:].rearrange("p a b -> p (a b)"), in_=DENS[dp][:].rearrange("p a b -> p (a b)"),
                    func=AF.Exp, scale=-1.0),
                       reads=[("DENS", dp)], writes=[("RR", dp)])
                P.emit("pool", lambda e, dp=dp, b=b: e.tensor_tensor(
                    out=RR[dp][:], in0=RR[dp][:], in1=SZA[:, :, b * 128:(b + 1) * 128], op=ALU.mult),
                       reads=[("RR", dp), "SZA"], writes=[("RR", dp)])
                P.emit("dve", lambda e, by=by, dp=dp, b=b: e.tensor_tensor(
                    out=YA[:, :, b * 128:(b + 1) * 128], in0=PSB[by][:].rearrange("p (a b) -> p a b", a=4),
                    in1=RR[dp][:], op=ALU.mult),
                       reads=[("PS", by), ("RR", dp)], writes=["YA"])

            def u_tile(g):
                ch, pos = [(3, 2), (3, 3), (4, 0), (4, 1)][g]
                fm_tile(it, ch, pos, lambda bi, g=g: act_evac(bi, lambda g=g: U[:, g, :], AF.Gelu, l, 9 + g, ["U"]))
                if g == 1:
                    wrelease(it, 3)

            def zg_tile(g):
                ch, pos = [(4, 2), (4, 3), (5, 0), (5, 1)][g]
                zp = 0

                def ev(bi, g=g, zp=zp):
                    act_evac(bi, lambda: SZG[zp][:], AF.Silu, l, 13 + g, [("SZG", zp)])
                    P.emit("pool", lambda e: e.tensor_tensor(out=U[:, g, :], in0=U[:, g, :], in1=SZG[zp][:],
                                                             op=ALU.mult),
                           reads=["U", ("SZG", zp)], writes=["U"])
                fm_tile(it, ch, pos, ev)
                if g == 1:
                    wrelease(it, 4)

            def gmlp_c(b):
                bm = next_bank()
                for g in range(4):
                    P.emit("pe", lambda e, bm=bm, g=g, b=b, l=l: e.matmul(
                        PSB[bm][:, g * 128:(g + 1) * 128], lhsT=VN[:, b, g * 128:(g + 1) * 128],
                        rhs=WST[:, l, g, :], start=True, stop=True),
                           reads=["VN", "WST"], writes=[("PS", bm)], sig=(g == 3))
                tp = 0
                P.emit("dve", lambda e, bm=bm, tp=tp: e.tensor_tensor(
                    out=T1[tp][:].rearrange("p a b -> p (a b)"), in0=PSB[bm][:], in1=PB1[:, O_BS:O_BS + 512],
                    op=ALU.add),
                       reads=[("PS", bm), "PB1"], writes=[("T1", tp)])
                P.emit("pool", lambda e, tp=tp, b=b: e.tensor_tensor(
                    out=YG[:, :, b * 128:(b + 1) * 128], in0=T1[tp][:], in1=U[:, :, b * 128:(b + 1) * 128],
                    op=ALU.mult),
                       reads=[("T1", tp), "U"], writes=["YG"])
            flush_b(1)
            kb0 = att_scores(0)
            for j in range(4):
                za_tile(j)
            vg_ln(*pend_ln.pop(0))
            flush_b(1)
            kb1 = att_scores(1)
            att_pv(0, kb0)
            for g in range(4):
                u_tile(g)
            vg_ln(*pend_ln.pop(0))
            flush_b(1)
            kb2 = att_scores(2)
            att_pv(1, kb1)
            for g in range(4):
                zg_tile(g)
            wrelease(it, 5)
            flush_b(1)
            kb3 = att_scores(3)
            att_pv(2, kb2)
            for b in range(NB):
                gmlp_c(b)
            att_pv(3, kb3)
            flush_b(99)
            if nxt_l is not None and nxt_l != l:
                load_pb1(nxt_l)
            P.emit("pool", lambda e, l=l, KTC=KTC: e.tensor_copy(out=KTC[:, l, :], in_=KT[:, 512:640]),
                   reads=["KT"], writes=[("KTC", s_)])
            P.emit("pool", lambda e, l=l, VAC=VAC: e.tensor_copy(out=VAC[:, l, :], in_=VA[:, 4, :]),
                   reads=["VA"], writes=[("VAC", s_)])
            P.emit("pool", lambda e, l=l, VBC=VBC: e.tensor_copy(out=VBC[:, l, :], in_=VB[:, 4, :]),
                   reads=["VB"], writes=[("VBC", s_)])
            sba = wslot(it, 6)
            sbg = wslot(it, 7)
            gps = {}

            def gates(c):
                gch = 8 + c // 2
                gp = rotv("sg")
                gps[c] = gp
                fm_tile(it, gch, (c % 2) * 2,
                        lambda bi, gp=gp, c=c: act_evac(bi, lambda: SGA[gp][:], AF.Sigmoid, l, 17 + c, [("SGA", gp)]))
                fm_tile(it, gch, (c % 2) * 2 + 1,
                        lambda bi, gp=gp, c=c: act_evac(bi, lambda: SGG[gp][:], AF.Sigmoid, l, 25 + c, [("SGG", gp)]))
                if c % 2 == 1:
                    wrelease(it, gch)

            early_f = (it + 1 < n_iter) and (order[it + 1][2] != s_)
            if early_f:
                emit_xload(it + 1)
                f_done.add(it + 1)
            gates(0)
            for c in range(8):
                if c + 1 < 8:
                    gates(c + 1)
                if early_f and c == 4:
                    emit_F_cast(it + 1, 0)
                if early_f and c == 5:
                    emit_F_cast(it + 1, 1)
                if early_f and c == 6:
                    emit_F_tr(it + 1, 0)
                    emit_F_cast(it + 1, 2)
                if early_f and c == 7:
                    emit_F_tr(it + 1, 1)
                    emit_F_cast(it + 1, 3)
                gp = gps[c]
                ba = next_bank()
                for kc in range(4):
                    P.emit("pe", lambda e, ba=ba, kc=kc, c=c, sba=sba: e.matmul(
                        PSB[ba][:], lhsT=WR[sba][:, kc * 1024 + c * 128: kc * 1024 + (c + 1) * 128],
                        rhs=YA[:, kc, :], start=(kc == 0), stop=(kc == 3)),
                           reads=[("WR", sba), "YA"], writes=[("PS", ba)], sig=(kc == 3))
                bgk = next_bank()
                for kc in range(4):
                    P.emit("pe", lambda e, bgk=bgk, kc=kc, c=c, sbg=sbg: e.matmul(
                        PSB[bgk][:], lhsT=WR[sbg][:, kc * 1024 + c * 128: kc * 1024 + (c + 1) * 128],
                        rhs=YG[:, kc, :], start=(kc == 0), stop=(kc == 3)),
                           reads=[("WR", sbg), "YG"], writes=[("PS", bgk)], sig=(kc == 3))
                mp = 0
                P.emit("dve", lambda e, ba=ba, mp=mp, gp=gp: e.tensor_tensor(
                    out=M1[mp][:], in0=PSB[ba][:], in1=SGA[gp][:], op=ALU.mult),
                       reads=[("PS", ba), ("SGA", gp)], writes=[("M1", mp)])
                P.emit("dve", lambda e, bgk=bgk, mp=mp, gp=gp: e.tensor_tensor(
                    out=M2[mp][:], in0=PSB[bgk][:], in1=SGG[gp][:], op=ALU.mult),
                       reads=[("PS", bgk), ("SGG", gp)], writes=[("M2", mp)])
                P.emit("pool", lambda e, mp=mp, c=c: e.tensor_tensor(
                    out=MG[:, c, :], in0=M1[mp][:], in1=M2[mp][:], op=ALU.add),
                       reads=[("M1", mp), ("M2", mp)], writes=["MG"])
            wrelease(it, 6)
            wrelease(it, 7)
            so = [wslot(it, 12), wslot(it, 13)]
            if early_f:
                emit_F_tr(it + 1, 2)
                emit_F_tr(it + 1, 3)
            for b in range(NB):
                for h in range(2):
                    bo = next_bank()
                    for kc in range(KC):
                        P.emit("pe", lambda e, bo=bo, kc=kc, b=b, h=h, soh=so[h]: e.matmul(
                            PSB[bo][:], lhsT=MG[:, kc, b * 128:(b + 1) * 128],
                            rhs=WR[soh][:, kc * 512:(kc + 1) * 512], start=(kc == 0), stop=(kc == KC - 1)),
                               reads=["MG", ("WR", so[h])], writes=[("PS", bo)], sig=(kc == KC - 1))
                    rp = rotv("rt")
                    P.emit("dve", lambda e, bo=bo, rp=rp, h=h: e.tensor_tensor(
                        out=RT[rp][:], in0=PSB[bo][:], in1=PB2[:, O_BOUT + h * 512:O_BOUT + (h + 1) * 512],
                        op=ALU.add),
                           reads=[("PS", bo), "PB2"], writes=[("RT", rp)])
                    P.emit("dve", lambda e, rp=rp, b=b, h=h, X=X: e.scalar_tensor_tensor(
                        out=X[:, b, h * 512:(h + 1) * 512], in0=X[:, b, h * 512:(h + 1) * 512], scalar=ALPHA,
                        in1=RT[rp][:], op0=ALU.mult, op1=ALU.add),
                           reads=[("X", s_, b), ("RT", rp)], writes=[("X", s_, b)])
                for h in range(2):
                    P.emit("dve", lambda e, b=b, h=h, X=X: e.bn_stats(out=ST6E[b][:, h * 6:(h + 1) * 6],
                                                                      in_=X[:, b, h * 512:(h + 1) * 512]),
                           reads=[("X", s_, b)], writes=[("ST6E", b)])

                def tail(b=b, X=X, s_=s_, tok0=tok0, last_layer=last_layer):
                    par = rotv("stat")
                    P.emit("dve", lambda e: e.bn_aggr(out=MV[par][:], in_=ST6E[b][:, 0:12]),
                           reads=[("ST6E", b)], writes=[("MV", par)])
                    P.emit("pool", lambda e: e.tensor_scalar(out=SD[par][:], in0=MV[par][:, 1:2], scalar1=LN_EPS,
                                                             scalar2=None, op0=ALU.add),
                           reads=[("MV", par)], writes=[("SD", par)])
                    P.emit("pool", lambda e: e.tensor_tensor(out=RS[par][:], in0=SD[par][:], in1=NEGH[:], op=ALU.pow),
                           reads=[("SD", par), "NEGH"], writes=[("RS", par)])
                    P.emit("dve", lambda e: e.scalar_tensor_tensor(out=NM[par][:], in0=MV[par][:, 0:1], scalar=-1.0,
                                                                   in1=RS[par][:], op0=ALU.mult, op1=ALU.mult),
                           reads=[("MV", par), ("RS", par)], writes=[("NM", par)])
                    P.emit("act", lambda e: e.activation(
                        out=X[:, b, :], in_=X[:, b, :], func=AF.Identity, scale=RS[par][:], bias=NM[par][:]),
                           reads=[("X", s_, b), ("RS", par), ("NM", par)], writes=[("X", s_, b)])

                def tail_b(b=b, X=X, s_=s_, tok0=tok0, last_layer=last_layer):
                    P.emit("pool", lambda e: e.tensor_tensor(
                        out=X[:, b, :], in0=X[:, b, :], in1=PB2[:, O_LNG:O_LNG + 1024], op=ALU.mult),
                           reads=[("X", s_, b), "PB2"], writes=[("X", s_, b)])
                    P.emit("pool", lambda e: e.tensor_tensor(
                        out=X[:, b, :], in0=X[:, b, :], in1=PB2[:, O_LNB:O_LNB + 1024], op=ALU.add),
                           reads=[("X", s_, b), "PB2"], writes=[("X", s_, b)])
                    if last_layer:
                        op = P.emit("sp", lambda e: e.dma_start(
                            out=out_d[tok0 + b * 128: tok0 + (b + 1) * 128, :], in_=X[:, b, :]),
                                    reads=[("X", s_, b)], dma=f"o{s_}_{b}")
                        store_toks.append(op.tok)
                deferred.append(tail)
                deferred_b.append(tail_b)
            wrelease(it, 12)
            wrelease(it, 13)
            if nxt_l is not None and nxt_l != l:
                deferred_b.append(lambda nxt_l=nxt_l: load_pb2(nxt_l))
        flush_deferred()
        for tok in store_toks:
            P.wait_tok("sp", tok)

        with nc.Block() as block:
            @block.tensor
            def _(e):
                for f in P.streams["pe"]:
                    f(e)

            @block.scalar
            def _(e):
                for f in P.streams["act"]:
                    f(e)

            @block.vector
            def _(e):
                for f in P.streams["dve"]:
                    f(e)

            @block.gpsimd
            def _(e):
                for f in P.streams["pool"]:
                    f(e)

            @block.sync
            def _(e):
                for f in P.streams["sp"]:
                    f(e)
    return nc


def _tile_cols():
    def q_tile(base, j):
        return list(range(base + j * 64, base + (j + 1) * 64)) + list(range(base + (4 + j) * 64, base + (5 + j) * 64))

    def nat(base, t):
        return list(range(base + t * 128, base + (t + 1) * 128))
    Q0, K0, V0, ZA0, U0, VG0, ZG0, GA0, GG0 = 0, 512, 640, 768, 1280, 1792, 2304, 2816, 3840
    tiles = []
    tiles += [q_tile(Q0, j) for j in range(4)]
    tiles += [nat(K0, 0), nat(V0, 0), q_tile(ZA0, 0), q_tile(ZA0, 1)]
    tiles += [nat(VG0, t) for t in range(4)]
    tiles += [q_tile(ZA0, 2), q_tile(ZA0, 3), nat(U0, 0), nat(U0, 1)]
    tiles += [nat(U0, 2), nat(U0, 3), nat(ZG0, 0), nat(ZG0, 1)]
    tiles += [nat(ZG0, 2), nat(ZG0, 3), None, None]
    return tiles


def _gate_tiles():
    GA0, GG0 = 2816, 3840
    tiles = []
    for c in range(8):
        tiles.append(list(range(GA0 + c * 128, GA0 + (c + 1) * 128)))
        tiles.append(list(range(GG0 + c * 128, GG0 + (c + 1) * 128)))
    return tiles


def pack_host(inputs, depth):
    w_in = np.asarray(inputs["w_in"], np.float32)
    b_in = np.asarray(inputs["b_in"], np.float32)
    sinks = np.asarray(inputs["attn_sinks"], np.float32)
    w_ba = np.asarray(inputs["w_branch_attn"], np.float32)
    w_bg = np.asarray(inputs["w_branch_gmlp"], np.float32)
    w_out = np.asarray(inputs["w_out"], np.float32)
    wpk = np.zeros((depth, NCHUNK, 128, CH), np.float32)
    pcol = np.zeros((128, depth, PCW), np.float32)
    prow = np.zeros((depth, PROW), np.float32)
    tiles = _tile_cols()
    gtiles = _gate_tiles()
    attn_rows = []
    for j in range(4):
        attn_rows += list(range(j * 64, (j + 1) * 64)) + list(range((4 + j) * 64, (5 + j) * 64))
    attn_rows = np.array(attn_rows)
    fm_bias_tiles = ([tiles[j] for j in range(4)] + [tiles[4]] + [tiles[6], tiles[7], tiles[12], tiles[13]] +
                     [tiles[14], tiles[15], tiles[16], tiles[17]] + [tiles[18], tiles[19], tiles[20], tiles[21]] +
                     [gtiles[2 * c] for c in range(8)] + [gtiles[2 * c + 1] for c in range(8)])
    for l in range(depth):
        wl = w_in[l].reshape(KC, 128, -1)
        for ch in range(6):
            blk = wpk[l, ch].reshape(128, KC, 512)
            for pos in range(4):
                cols = tiles[ch * 4 + pos]
                if cols is None:
                    continue
                blk[:, :, pos * 128:(pos + 1) * 128] = wl[:, :, cols].transpose(1, 0, 2)
        for ch in range(4):
            blk = wpk[l, 8 + ch].reshape(128, KC, 512)
            for pos in range(4):
                cols = gtiles[ch * 4 + pos]
                blk[:, :, pos * 128:(pos + 1) * 128] = wl[:, :, cols].transpose(1, 0, 2)
        wpk[l, 6].reshape(128, 4, 1024)[:] = w_ba[l][attn_rows].reshape(4, 128, 1024).transpose(1, 0, 2)
        wpk[l, 7].reshape(128, 4, 1024)[:] = w_bg[l].reshape(4, 128, 1024).transpose(1, 0, 2)
        wo = w_out[l].reshape(KC, 128, 1024)
        wpk[l, 12].reshape(128, KC, 512)[:] = wo[:, :, 0:512].transpose(1, 0, 2)
        wpk[l, 13].reshape(128, KC, 512)[:] = wo[:, :, 512:1024].transpose(1, 0, 2)
        for i, cols in enumerate(fm_bias_tiles):
            pcol[:, l, i] = b_in[l][cols]
        for j in range(4):
            pcol[0:64, l, 33 + j] = sinks[l, j]
            pcol[64:128, l, 33 + j] = sinks[l, 4 + j]
        prow[l, O_BV:O_BV + 128] = b_in[l][640:768]
        prow[l, O_BVG:O_BVG + 512] = b_in[l][1792:2304]
        prow[l, O_GLG:O_GLG + 512] = inputs["gmlp_ln_g"][l]
        prow[l, O_GLB:O_GLB + 512] = inputs["gmlp_ln_b"][l]
        prow[l, O_BS:O_BS + 512] = np.asarray(inputs["b_spatial"][l], np.float32).reshape(512)
        prow[l, PB1W + O_BOUT:PB1W + O_BOUT + 1024] = inputs["b_out"][l]
        prow[l, PB1W + O_LNG:PB1W + O_LNG + 1024] = inputs["ln_g"][l]
        prow[l, PB1W + O_LNB:PB1W + O_LNB + 1024] = inputs["ln_b"][l]
    wsp = np.ascontiguousarray(np.asarray(inputs["w_spatial"], np.float32)[:depth].transpose(0, 3, 1, 2)).reshape(depth * 128, 512)
    return (wpk.reshape(depth * NCHUNK * 128, CH), np.ascontiguousarray(prow),
            np.ascontiguousarray(pcol.reshape(128, depth * PCW)), wsp)


def const_tables():
    cst = np.zeros((128, CSTW), np.float32)
    k = np.arange(128)[:, None]
    q = np.arange(128)[None, :]
    bias = np.zeros((128, 2, 8, 128), np.float32)
    for h in range(8):
        slope = 2.0 ** (-(h + 1))
        dist0 = (q - k + 128).astype(np.float32)
        bias[:, 0, h, :] = np.where(k > q, -8.0 * slope * dist0, -1.0e6)
        dist1 = (q - k).astype(np.float32)
        bias[:, 1, h, :] = np.where(k <= q, -8.0 * slope * dist1, -1.0e6)
    cst[:, 0:2048] = bias.reshape(128, 2048)
    cst[:, 2048:2176] = (k <= q).astype(np.float32)
    cst[:, 2176:2304] = np.eye(128, dtype=np.float32)
    return cst


_NC_CACHE = {}


def run(inputs, n_cores, depth):
    x = np.asarray(inputs["x"], np.float32)
    B, S, _ = x.shape
    assert B % n_cores == 0
    nseq = B // n_cores
    key = (nseq, S, depth)
    if key not in _NC_CACHE:
        _NC_CACHE[key] = build_program(nseq, S, depth)
    nc = _NC_CACHE[key]
    wpk, prow, pcol, wsp = pack_host(inputs, depth)
    cst = const_tables()
    xs = x.reshape(n_cores, nseq * S, D)
    in_maps = [{"x": xs[c], "wpk": wpk, "prow": prow, "pcol": pcol, "wsp": wsp, "cst": cst} for c in range(n_cores)]
    res = run_bass_kernel_spmd(nc, in_maps, core_ids=list(range(n_cores)))
    out = np.stack([res.results[c]["out"] for c in range(n_cores)], axis=0)
    return out.reshape(B, S, D).astype(np.float32)


def kernel(x, w_in, b_in, attn_sinks, gmlp_ln_g, gmlp_ln_b, w_spatial, b_spatial,
           w_branch_attn, w_branch_gmlp, w_out, b_out, ln_g, ln_b):
    inputs = dict(x=x, w_in=w_in, b_in=b_in, attn_sinks=attn_sinks, gmlp_ln_g=gmlp_ln_g, gmlp_ln_b=gmlp_ln_b,
                  w_spatial=w_spatial, b_spatial=b_spatial, w_branch_attn=w_branch_attn,
                  w_branch_gmlp=w_branch_gmlp, w_out=w_out, b_out=b_out, ln_g=ln_g, ln_b=ln_b)
    inputs = {k: np.asarray(v) for k, v in inputs.items()}
    return run(inputs, 8, 4)
```

```python
import contextlib
import numpy as np
import concourse.bass as bass
import concourse.mybir as mybir
from concourse.bass_utils import run_bass_kernel_spmd

F32 = mybir.dt.float32
BF16 = mybir.dt.bfloat16
AF = mybir.ActivationFunctionType
ALU = mybir.AluOpType

D = 1024
KC = 8
NB = 4
SBT = 512
NCHUNK = 14
CH = 4096
NW = 5
ALPHA = 8.0 ** 0.25
LN_EPS = 1e-5
PCW = 40
PROW = 5248
O_BV, O_BVG, O_GLG, O_GLB, O_BS = 0, 128, 640, 1152, 1664
PB1W = 2176
O_BOUT, O_LNG, O_LNB = 0, 1024, 2048
PB2W = 3072
CSTW = 2048 + 128 + 128
FOLD_WAITS = True


class Op:
    __slots__ = ("eng", "tok")

    def __init__(self, eng):
        self.eng = eng
        self.tok = None


class Planner:
    ENGS = ("pe", "act", "dve", "pool", "sp")

    def __init__(self, nc, stack):
        self.nc = nc
        self.stack = stack
        self.streams = {e: [] for e in self.ENGS}
        self.waited = {e: {} for e in self.ENGS}
        self.lastw = {}
        self.readers = {}
        self.esem = {}
        self.ecnt = {}
        for e in ("pe", "act", "dve", "pool"):
            self.esem[e] = stack.enter_context(nc.semaphore("es_" + e))
            self.ecnt[e] = 0
        self.pending = {e: [] for e in ("pe", "act", "dve", "pool")}
        self.dsem = {}
        self.dval = {}
        self.semkey = {}

    def dma_sem(self, name):
        if name not in self.dsem:
            self.dsem[name] = self.stack.enter_context(self.nc.semaphore("ds_" + name))
            self.dval[name] = 0
        return name

    def emit(self, eng, fn, reads=(), writes=(), sig=True, dma=None, nofold=False):
        deps = []
        for k in reads:
            w = self.lastw.get(k)
            if w is not None:
                deps.append(w)
        for k in writes:
            w = self.lastw.get(k)
            if w is not None:
                deps.append(w)
            deps.extend(self.readers.get(k, ()))
        stream = self.streams[eng]
        wd = self.waited[eng]
        ws = []
        for d in deps:
            if d.eng == "pe" and eng == "pe":
                continue
            assert d.tok is not None, "dependency on unsignalled op"
            sname, val = d.tok
            if wd.get(sname, 0) >= val:
                continue
            wd[sname] = val
            ws = [w for w in ws if w[0] != sname]
            ws.append((sname, val))
        ws = [(self._sem(n), v) for n, v in ws]
        fold = None
        if ws and dma is None and FOLD_WAITS and not nofold:
            fold = ws.pop()
        for sem, val in ws:
            stream.append(lambda e, sem=sem, val=val: e.wait_ge(sem, val))
        op = Op(eng)
        if dma is not None:
            self.dval[dma] += 16
            op.tok = (dma, self.dval[dma])
            sem = self.dsem[dma]
            stream.append(lambda e, fn=fn, sem=sem: fn(e).then_inc(sem, 16))
        else:
            isem = None
            if sig:
                self.ecnt[eng] += 1
                op.tok = ("es_" + eng, self.ecnt[eng])
                for p in self.pending[eng]:
                    p.tok = op.tok
                self.pending[eng] = []
                isem = self.esem[eng]
            else:
                self.pending[eng].append(op)

            def run(e, fn=fn, fold=fold, isem=isem):
                inst = fn(e)
                if fold is not None:
                    inst.wait_op(fold[0], fold[1], "sem-ge")
                if isem is not None:
                    inst.then_inc(isem, 1)
            stream.append(run)
        for k in writes:
            self.lastw[k] = op
            self.readers[k] = []
        for k in reads:
            if k not in writes:
                self.readers.setdefault(k, []).append(op)
        return op

    def _sem(self, sname):
        if sname.startswith("es_"):
            return self.esem[sname[3:]]
        return self.dsem[sname]

    def wait_tok(self, eng, tok):
        sname, val = tok
        wd = self.waited[eng]
        if wd.get(sname, 0) >= val:
            return
        wd[sname] = val
        sem = self._sem(sname)
        self.streams[eng].append(lambda e, sem=sem, val=val: e.wait_ge(sem, val))


def build_program(nseq, seq, depth):
    nsb_seq = seq // SBT
    nsb = nseq * nsb_seq
    ntok = nseq * seq
    nc = bass.Bass("TRN2", target_bir_lowering=False)
    x_d = nc.dram_tensor("x", [ntok, D], F32, kind="ExternalInput").ap()
    wpk_d = nc.dram_tensor("wpk", [depth * NCHUNK * 128, CH], F32, kind="ExternalInput").ap()
    prow_d = nc.dram_tensor("prow", [depth, PROW], F32, kind="ExternalInput").ap()
    pcol_d = nc.dram_tensor("pcol", [128, depth * PCW], F32, kind="ExternalInput").ap()
    wsp_d = nc.dram_tensor("wsp", [depth * 128, 512], F32, kind="ExternalInput").ap()
    cst_d = nc.dram_tensor("cst", [128, CSTW], F32, kind="ExternalInput").ap()
    out_d = nc.dram_tensor("out", [ntok, D], F32, kind="ExternalOutput").ap()
    wbf_d = nc.dram_tensor("wbf", [depth * NCHUNK * 128, CH], BF16, kind="Internal").ap()

    with contextlib.ExitStack() as stack:
        def sb(name, shape, dt):
            return stack.enter_context(nc.sbuf_tensor(name, shape, dt))

        def ps(name, shape, dt):
            return stack.enter_context(nc.psum_tensor(name, shape, dt))

        XS = [sb(f"X{i}", [128, NB, D], F32) for i in range(nseq)]
        XB = [sb(f"XB{i}", [128, D], BF16) for i in range(2)]
        XT = sb("XT", [128, KC, SBT], BF16)
        WR = [sb(f"WR{i}", [128, CH], BF16) for i in range(NW)]
        QT = sb("QT", [128, 4, SBT], BF16)
        KT = sb("KT", [128, 5 * 128], BF16)
        KTCS = [sb(f"KTC{i}", [128, depth, 128], BF16) for i in range(nseq)]
        VA = sb("VA", [128, 5, 128], BF16)
        VB = sb("VB", [128, 5, 128], BF16)
        VACS = [sb(f"VAC{i}", [128, depth, 128], BF16) for i in range(nseq)]
        VBCS = [sb(f"VBC{i}", [128, depth, 128], BF16) for i in range(nseq)]
        SZA = sb("SZA", [128, 4, SBT], F32)
        U = sb("U", [128, 4, SBT], F32)
        SZG = [sb(f"SZG{i}", [128, SBT], F32) for i in range(1)]
        VN = sb("VN", [128, NB, 512], BF16)
        YA = sb("YA", [128, 4, SBT], BF16)
        YG = sb("YG", [128, 4, SBT], BF16)
        MG = sb("MG", [128, 8, SBT], BF16)
        SGA = [sb(f"SGA{i}", [128, SBT], F32) for i in range(2)]
        SGG = [sb(f"SGG{i}", [128, SBT], F32) for i in range(2)]
        PT = [sb(f"PT{i}", [128, 2, 2, SBT], BF16) for i in range(2)]
        BIAS8 = sb("BIAS8", [128, 2, 8, 128], F32)
        PB1 = sb("PB1", [128, PB1W], F32)
        PB2 = sb("PB2", [128, PB2W], F32)
        PC = sb("PC", [128, depth, PCW], F32)
        SE = sb("SE", [128, depth, 4], F32)
        WST = sb("WST", [128, depth, 4, 128], BF16)
        WMASK = sb("WMASK", [128, 128], F32)
        IDF = sb("IDF", [128, 128], F32)
        IDENT = sb("IDENT", [128, 128], BF16)
        ONESA = sb("ONESA", [128, 128], BF16)
        ONESB = sb("ONESB", [128, 128], BF16)
        DENS = [sb(f"DENS{i}", [128, 4, 128], F32) for i in range(1)]
        RR = [sb(f"RR{i}", [128, 4, 128], F32) for i in range(1)]
        T1 = [sb(f"T1{i}", [128, 4, 128], F32) for i in range(1)]
        M1 = [sb(f"M1{i}", [128, SBT], F32) for i in range(1)]
        M2 = [sb(f"M2{i}", [128, SBT], F32) for i in range(1)]
        VGF = [sb(f"VGF{i}", [128, 512], F32) for i in range(3)]
        RT = [sb(f"RT{i}", [128, 512], F32) for i in range(2)]
        ST6 = [sb(f"ST6{i}", [128, 12], F32) for i in range(2)]
        MV = [sb(f"MV{i}", [128, 2], F32) for i in range(2)]
        SD = [sb(f"SD{i}", [128, 1], F32) for i in range(2)]
        RS = [sb(f"RS{i}", [128, 1], F32) for i in range(2)]
        NM = [sb(f"NM{i}", [128, 1], F32) for i in range(2)]

        NPS = 8
        PSB = [ps(f"PS{i}", [128, 512], F32) for i in range(NPS)]

        print('sbuf bytes remaining', nc.sbuf_bytes_remaining)
        P = Planner(nc, stack)
        rot = {"ps": 0}

        def next_bank():
            i = rot["ps"]
            rot["ps"] = (i + 1) % NPS
            return i

        def rotv(name):
            v = rot.get(name, 0)
            rot[name] = v ^ 1
            return v

        for i in range(4):
            P.dma_sem(f"cst{i}")
        P.emit("sp", lambda e: e.dma_start(out=BIAS8[:].rearrange("p a b c -> p (a b c)"), in_=cst_d[:, 0:2048]),
               writes=["BIAS8"], dma="cst0")
        P.emit("sp", lambda e: e.dma_start(out=WMASK[:], in_=cst_d[:, 2048:2176]), writes=["WMASK"], dma="cst1")
        P.emit("sp", lambda e: e.dma_start(out=IDF[:], in_=cst_d[:, 2176:2304]), writes=["IDF"], dma="cst2")
        P.emit("sp", lambda e: e.dma_start(out=PC[:].rearrange("p a b -> p (a b)"), in_=pcol_d[:, :]),
               writes=["PC"], dma="cst3")
        conv_tok = {}

        def emit_conv(l):
            for c in range(NCHUNK):
                name = P.dma_sem(f"cv{l}_{c}" if l == 0 else f"cv{l}")
                r0 = (l * NCHUNK + c) * 128
                op = P.emit("pool", lambda e, r0=r0: e.dma_start(out=wbf_d[r0:r0 + 128, :], in_=wpk_d[r0:r0 + 128, :],
                                                                 max_dma_last_dim=8192),
                            writes=[("wbf", l, c)], dma=name)
                conv_tok[(l, c)] = op
            if l > 0:
                last = conv_tok[(l, NCHUNK - 1)].tok
                for c in range(NCHUNK):
                    conv_tok[(l, c)].tok = last

        emit_conv(0)

        P.emit("dve", lambda e: e.tensor_copy(out=IDENT[:], in_=IDF[:]), reads=["IDF"], writes=["IDENT"])
        P.emit("dve", lambda e: e.memset(ONESA[:, 0:64], 1.0), writes=["ONESA"])
        P.emit("dve", lambda e: e.memset(ONESA[:, 64:128], 0.0), writes=["ONESA"])
        P.emit("dve", lambda e: e.memset(ONESB[:, 0:64], 0.0), writes=["ONESB"])
        P.emit("dve", lambda e: e.memset(ONESB[:, 64:128], 1.0), writes=["ONESB"])
        P.emit("dve", lambda e: e.memset(VA[:].rearrange("p a b -> p (a b)"), 0.0), writes=["VA"])
        P.emit("dve", lambda e: e.memset(VB[:].rearrange("p a b -> p (a b)"), 0.0), writes=["VB"])
        for i in range(nseq):
            P.emit("dve", lambda e, i=i: e.memset(VACS[i][:].rearrange("p a b -> p (a b)"), 0.0), writes=[("VAC", i)])
            P.emit("dve", lambda e, i=i: e.memset(VBCS[i][:].rearrange("p a b -> p (a b)"), 0.0), writes=[("VBC", i)])
            P.emit("dve", lambda e, i=i: e.memset(KTCS[i][:].rearrange("p a b -> p (a b)"), 0.0), writes=[("KTC", i)])
        P.emit("act", lambda e: e.activation(out=SE[:], in_=PC[:, :, 33:37], func=AF.Exp), reads=["PC"], writes=["SE"])
        P.dma_sem("wsf")

        def emit_wsp():
            for l in range(depth):
                P.emit("sp", lambda e, l=l: e.dma_start(out=VGF[0][:], in_=wsp_d[l * 128:(l + 1) * 128, :]),
                       writes=[("VGF", 0)], dma="wsf")
                for g in range(4):
                    P.emit("dve", lambda e, l=l, g=g: e.tensor_tensor(
                        out=WST[:, l, g, :], in0=VGF[0][:, g * 128:(g + 1) * 128], in1=WMASK[:], op=ALU.mult),
                           reads=[("VGF", 0), "WMASK"], writes=["WST"])

        chunk_seq = [(0, l, c) for p_ in range(nsb_seq) for l in range(depth) for s_ in range(nseq) for c in range(NCHUNK)]
        wstate = {"next": 0, "free": list(range(NW)), "slot": {}}
        for i in range(NW):
            P.dma_sem(f"wr{i}")

        def prefetch():
            while wstate["free"] and wstate["next"] < len(chunk_seq):
                n = wstate["next"]
                sbi, l, c = chunk_seq[n]
                if (l, c) not in conv_tok:
                    break
                slot = wstate["free"].pop(0)
                wstate["slot"][n] = slot
                wstate["next"] = n + 1
                P.wait_tok("sp", conv_tok[(l, c)].tok)
                r0 = (l * NCHUNK + c) * 128
                P.emit("sp", lambda e, slot=slot, r0=r0: e.dma_start(out=WR[slot][:], in_=wbf_d[r0:r0 + 128, :]),
                       writes=[("WR", slot)], dma=f"wr{slot}")

        def wslot(it, c):
            n = it * NCHUNK + c
            if n not in wstate["slot"]:
                prefetch()
            assert n in wstate["slot"], f"weight chunk {n} not loaded (ring too small?)"
            return wstate["slot"][n]

        def wrelease(it, c):
            n = it * NCHUNK + c
            wstate["free"].append(wstate["slot"][n])
            prefetch()

        def fm_tile(it, chunk, pos, evac):
            slot = wslot(it, chunk)
            bi = next_bank()
            for kc in range(KC):
                P.emit("pe", lambda e, bi=bi, slot=slot, kc=kc, pos=pos: e.matmul(
                    PSB[bi][:], lhsT=WR[slot][:, kc * 512 + pos * 128: kc * 512 + pos * 128 + 128],
                    rhs=XT[:, kc, :], start=(kc == 0), stop=(kc == KC - 1)),
                       reads=[("WR", slot), "XT"], writes=[("PS", bi)], sig=(kc == KC - 1))
            evac(bi)

        def act_evac(bi, dst_fn, func, l, col, writes):
            P.emit("act", lambda e: e.activation(out=dst_fn(), in_=PSB[bi][:], func=func, bias=PC[:, l, col:col + 1]),
                   reads=[("PS", bi), "PC"], writes=writes)

        P.dma_sem("pb1")
        P.dma_sem("pb2")
        for i in range(nseq):
            for b in range(NB):
                P.dma_sem(f"x{i}_{b}")
                P.dma_sem(f"o{i}_{b}")

        def load_pb1(l):
            P.emit("sp", lambda e: e.dma_start(out=PB1[:], in_=prow_d[l, 0:PB1W].partition_broadcast(128)),
                   writes=["PB1"], dma="pb1")

        def load_pb2(l):
            P.emit("sp", lambda e: e.dma_start(out=PB2[:], in_=prow_d[l, PB1W:PROW].partition_broadcast(128)),
                   writes=["PB2"], dma="pb2")

        def ln_stats(src_aps, par, reads):
            n = len(src_aps)
            for h, ap_fn in enumerate(src_aps):
                P.emit("dve", lambda e, h=h, ap_fn=ap_fn: e.bn_stats(out=ST6[par][:, h * 6:(h + 1) * 6], in_=ap_fn()),
                       reads=reads, writes=[("ST6", par)])
            P.emit("dve", lambda e: e.bn_aggr(out=MV[par][:], in_=ST6[par][:, 0:6 * n]),
                   reads=[("ST6", par)], writes=[("MV", par)])
            P.emit("pool", lambda e: e.tensor_scalar(out=SD[par][:], in0=MV[par][:, 1:2], scalar1=LN_EPS, scalar2=None,
                                                     op0=ALU.add),
                   reads=[("MV", par)], writes=[("SD", par)])
            P.emit("pool", lambda e: e.tensor_tensor(out=RS[par][:], in0=SD[par][:], in1=NEGH[:], op=ALU.pow),
                   reads=[("SD", par), "NEGH"], writes=[("RS", par)])
            P.emit("dve", lambda e: e.scalar_tensor_tensor(out=NM[par][:], in0=MV[par][:, 0:1], scalar=-1.0,
                                                           in1=RS[par][:], op0=ALU.mult, op1=ALU.mult),
                   reads=[("MV", par), ("RS", par)], writes=[("NM", par)])

        NEGH = sb("NEGH", [128, 1], F32)
        P.emit("dve", lambda e: e.memset(NEGH[:], -0.5), writes=["NEGH"])

        store_toks = []

        xl_done = set()
        f_done = set()
        deferred = []
        ST6E = [sb(f"ST6E{i}", [128, 12], F32) for i in range(NB)]

        deferred_b = []

        def flush_deferred():
            while deferred:
                deferred.pop(0)()
            while deferred_b:
                deferred_b.pop(0)()

        def flush_a():
            while deferred:
                deferred.pop(0)()

        def flush_b(n):
            while deferred_b and n > 0:
                deferred_b.pop(0)()
                n -= 1

        def emit_xload(it_n):
            p_n, l_n, s_n = order[it_n]
            if l_n != 0 or it_n in xl_done:
                return
            xl_done.add(it_n)
            Xn = XS[s_n]
            t0n = s_n * seq + p_n * SBT
            for b in range(NB):
                P.emit("sp", lambda e, b=b, t0n=t0n, Xn=Xn: e.dma_start(
                    out=Xn[:, b, :], in_=x_d[t0n + b * 128: t0n + (b + 1) * 128, :]),
                       writes=[("X", s_n, b)], dma=f"x{s_n}_{b}")

        fcast = {}

        def emit_F_cast(it_n, b):
            p_n, l_n, s_n = order[it_n]
            Xn = XS[s_n]
            xp = rotv("xb")
            fcast[(it_n, b)] = xp
            P.emit("act", lambda e, b=b, xp=xp, Xn=Xn: e.activation(out=XB[xp][:], in_=Xn[:, b, :], func=AF.Identity),
                   reads=[("X", s_n, b)], writes=[("XB", xp)])

        def emit_F_tr(it_n, b):
            xp = fcast[(it_n, b)]
            bi = next_bank()
            for kc in range(KC):
                P.emit("pe", lambda e, xp=xp, kc=kc, bi=bi: e.transpose(
                    out=PSB[bi][:].bitcast(BF16)[:, kc * 128:(kc + 1) * 128], in_=XB[xp][:, kc * 128:(kc + 1) * 128],
                    identity=IDENT[:]),
                       reads=[("XB", xp), "IDENT"], writes=[("PS", bi)], sig=(kc == KC - 1))
            P.emit("dve", lambda e, b=b, bi=bi: e.tensor_copy(
                out=XT[:, :, b * 128:(b + 1) * 128],
                in_=PSB[bi][:].bitcast(BF16).rearrange("p (a b) -> p a b", a=KC)),
                   reads=[("PS", bi)], writes=["XT"])

        def emit_F_block(it_n, b):
            emit_F_cast(it_n, b)
            emit_F_tr(it_n, b)

        n_iter = nsb_seq * depth * nseq
        order = [(p_, l_, s_) for p_ in range(nsb_seq) for l_ in range(depth) for s_ in range(nseq)]
        emit_xload(0)
        load_pb1(0)
        load_pb2(0)
        prefetch()
        emit_wsp()
        for it, (p_, l, s_) in enumerate(order):
            X = XS[s_]
            KTC, VAC, VBC = KTCS[s_], VACS[s_], VBCS[s_]
            tok0 = s_ * seq + p_ * SBT
            first_sb = (p_ == 0)
            last_layer = (l == depth - 1)
            nxt_l = order[it + 1][1] if it + 1 < n_iter else None
            if p_ == 0 and s_ == 0 and l + 1 < depth:
                emit_conv(l + 1)
            prefetch()
            emit_xload(it)
            P.emit("pool", lambda e, l=l, KTC=KTC: e.tensor_copy(out=KT[:, 0:128], in_=KTC[:, l, :]),
                   reads=[("KTC", s_)], writes=["KT"])
            P.emit("pool", lambda e, l=l, VAC=VAC: e.tensor_copy(out=VA[:, 0, :], in_=VAC[:, l, :]),
                   reads=[("VAC", s_)], writes=["VA"])
            P.emit("pool", lambda e, l=l, VBC=VBC: e.tensor_copy(out=VB[:, 0, :], in_=VBC[:, l, :]),
                   reads=[("VBC", s_)], writes=["VB"])
            if it not in f_done:
                flush_deferred()
            else:
                flush_a()
            if it not in f_done:
                f_done.add(it)
                for b in range(NB):
                    emit_F_block(it, b)
            for j in range(4):
                fm_tile(it, 0, j, lambda bi, j=j: act_evac(bi, lambda j=j: QT[:, j, :], AF.Identity, l, j, ["QT"]))
            wrelease(it, 0)
            fm_tile(it, 1, 0, lambda bi: act_evac(bi, lambda: KT[:, 128:640], AF.Identity, l, 4, ["KT"]))
            s1 = wslot(it, 1)
            s2 = wslot(it, 2)

            def vg_ln(vp, b):
                sp_ = rotv("stat")
                ln_stats([lambda vp=vp: VGF[vp][:]], sp_, [("VGF", vp)])
                P.emit("act", lambda e, vp=vp, sp_=sp_: e.activation(
                    out=VGF[vp][:], in_=VGF[vp][:], func=AF.Identity, scale=RS[sp_][:], bias=NM[sp_][:]),
                       reads=[("VGF", vp), ("RS", sp_), ("NM", sp_)], writes=[("VGF", vp)])
                P.emit("pool", lambda e, vp=vp: e.tensor_tensor(
                    out=VGF[vp][:], in0=VGF[vp][:], in1=PB1[:, O_GLG:O_GLG + 512], op=ALU.mult),
                       reads=[("VGF", vp), "PB1"], writes=[("VGF", vp)])
                P.emit("pool", lambda e, vp=vp, b=b: e.tensor_tensor(
                    out=VN[:, b, :], in0=VGF[vp][:], in1=PB1[:, O_GLB:O_GLB + 512], op=ALU.add),
                       reads=[("VGF", vp), "PB1"], writes=["VN"])

            pend_ln = []
            for b in range(NB):
                bg = next_bank()
                for kc in range(KC):
                    P.emit("pe", lambda e, bg=bg, kc=kc, b=b, s2=s2: e.matmul(
                        PSB[bg][:], lhsT=XT[:, kc, b * 128:(b + 1) * 128], rhs=WR[s2][:, kc * 512:(kc + 1) * 512],
                        start=(kc == 0), stop=(kc == KC - 1)),
                           reads=["XT", ("WR", s2)], writes=[("PS", bg)], sig=(kc == KC - 1))
                bv = next_bank()
                for kc in range(KC):
                    P.emit("pe", lambda e, bv=bv, kc=kc, b=b, s1=s1: e.matmul(
                        PSB[bv][:, 0:128], lhsT=XT[:, kc, b * 128:(b + 1) * 128],
                        rhs=WR[s1][:, kc * 512 + 128: kc * 512 + 256],
                        start=(kc == 0), stop=(kc == KC - 1)),
                           reads=["XT", ("WR", s1)], writes=[("PS", bv)], sig=(kc == KC - 1))
                P.emit("dve", lambda e, bv=bv, b=b: e.tensor_tensor(
                    out=VA[:, b + 1, 0:64], in0=PSB[bv][:, 0:64], in1=PB1[:, O_BV:O_BV + 64], op=ALU.add),
                       reads=[("PS", bv), "PB1"], writes=["VA"])
                P.emit("dve", lambda e, bv=bv, b=b: e.tensor_tensor(
                    out=VB[:, b + 1, 64:128], in0=PSB[bv][:, 64:128], in1=PB1[:, O_BV + 64:O_BV + 128], op=ALU.add),
                       reads=[("PS", bv), "PB1"], writes=["VB"])
                vp = b % 3
                P.emit("dve", lambda e, bg=bg, vp=vp: e.tensor_tensor(
                    out=VGF[vp][:], in0=PSB[bg][:], in1=PB1[:, O_BVG:O_BVG + 512], op=ALU.add),
                       reads=[("PS", bg), "PB1"], writes=[("VGF", vp)])
                P.emit("act", lambda e, vp=vp: e.activation(out=VGF[vp][:], in_=VGF[vp][:], func=AF.Gelu),
                       reads=[("VGF", vp)], writes=[("VGF", vp)])
                pend_ln.append((vp, b))
                if len(pend_ln) > 2:
                    vg_ln(*pend_ln.pop(0))
            wrelease(it, 2)
            def za_tile(j):
                ch, pos = [(1, 2), (1, 3), (3, 0), (3, 1)][j]
                fm_tile(it, ch, pos, lambda bi, j=j: act_evac(bi, lambda j=j: SZA[:, j, :], AF.Silu, l, 5 + j, ["SZA"]))
                if j == 1:
                    wrelease(it, 1)

            def att_scores(b):
                first_blk = first_sb and b == 0
                kbs = [1] if first_blk else [0, 1]
                for kb in kbs:
                    for g in range(2):
                        bi = next_bank()
                        ks = b + kb
                        P.emit("pe", lambda e, bi=bi, g=g, ks=ks, b=b: e.matmul(
                            PSB[bi][:].rearrange("p (a b) -> p a b", a=4),
                            lhsT=KT[g * 64:(g + 1) * 64, ks * 128:(ks + 1) * 128],
                            rhs=QT[g * 64:(g + 1) * 64, :, b * 128:(b + 1) * 128], start=True, stop=True),
                               reads=["KT", "QT"], writes=[("PS", bi)])
                        P.emit("dve", lambda e, bi=bi, g=g, kb=kb: e.tensor_tensor(
                            out=PSB[bi][:], in0=PSB[bi][:],
                            in1=BIAS8[:, kb, g * 4:(g + 1) * 4, :].rearrange("p a b -> p (a b)"), op=ALU.add),
                               reads=[("PS", bi), "BIAS8"], writes=[("PS", bi)])
                        pp = b % 2
                        P.emit("act", lambda e, bi=bi, g=g, kb=kb, pp=pp: e.activation(
                            out=PT[pp][:, g, kb, :], in_=PSB[bi][:], func=AF.Exp, scale=0.125),
                               reads=[("PS", bi)], writes=[("PT", pp, g)])
                return kbs

            def att_pv(b, kbs):
                pp = b % 2
                by = next_bank()
                bd = next_bank()
                terms = [(kb, g) for kb in kbs for g in range(2)]
                nkb = len(kbs)
                for i, (kb, g) in enumerate(terms):
                    vt = VA if g == 0 else VB
                    P.emit("pe", lambda e, by=by, g=g, kb=kb, vt=vt, b=b, pp=pp, first=(kb == kbs[0]), last=(kb == kbs[-1]): e.matmul(
                        PSB[by][g * 64:(g + 1) * 64, :], lhsT=vt[:, b + kb, g * 64:(g + 1) * 64], rhs=PT[pp][:, g, kb, :],
                        start=first, stop=last),
                           reads=["VA", "VB", ("PT", pp, g)], writes=[("PS", by)], sig=(i == len(terms) - 1))
                for i, (kb, g) in enumerate(terms):
                    P.emit("pe", lambda e, bd=bd, g=g, kb=kb, pp=pp, first=(kb == kbs[0]), last=(kb == kbs[-1]): e.matmul(
                        PSB[bd][g * 64:(g + 1) * 64, :], lhsT=ONESA[:, 0:64], rhs=PT[pp][:, g, kb, :],
                        start=first, stop=last),
                           reads=["ONESA", ("PT", pp, g)], writes=[("PS", bd)], sig=(i == len(terms) - 1))
                dp = 0
                for j in range(4):
                    P.emit("act", lambda e, bd=bd, j=j, dp=dp, l=l: e.activation(
                        out=DENS[dp][:, j, :], in_=PSB[bd][:, j * 128:(j + 1) * 128], func=AF.Ln,
                        bias=SE[:, l, j:j + 1]),
                           reads=[("PS", bd), "SE"], writes=[("DENS", dp)])
                P.emit("act", lambda e, dp=dp: e.activation(
                    out=RR[dp][:].rearrange("p a b -> p (a b)"), in_=DENS[dp][:].rearrange("p a b -> p (a b)"),
                    func=AF.Exp, scale=-1.0),
                       reads=[("DENS", dp)], writes=[("RR", dp)])
                P.emit("pool", lambda e, dp=dp, b=b: e.tensor_tensor(
                    out=RR[dp][:], in0=RR[dp][:], in1=SZA[:, :, b * 128:(b + 1) * 128], op=ALU.mult),
                       reads=[("RR", dp), "SZA"], writes=[("RR", dp)])
                P.emit("dve", lambda e, by=by, dp=dp, b=b: e.tensor_tensor(
                    out=YA[:, :, b * 128:(b + 1) * 128], in0=PSB[by][:].rearrange("p (a b) -> p a b", a=4),
                    in1=RR[dp][:], op=ALU.mult),
                       reads=[("PS", by), ("RR", dp)], writes=["YA"])

            def u_tile(g):
                ch, pos = [(3, 2), (3, 3), (4, 0), (4, 1)][g]
                fm_tile(it, ch, pos, lambda bi, g=g: act_evac(bi, lambda g=g: U[:, g, :], AF.Gelu, l, 9 + g, ["U"]))
                if g == 1:
                    wrelease(it, 3)

            def zg_tile(g):
                ch, pos = [(4, 2), (4, 3), (5, 0), (5, 1)][g]
                zp = 0

                def ev(bi, g=g, zp=zp):
                    act_evac(bi, lambda: SZG[zp][:], AF.Silu, l, 13 + g, [("SZG", zp)])
                    P.emit("pool", lambda e: e.tensor_tensor(out=U[:, g, :], in0=U[:, g, :], in1=SZG[zp][:],
                                                             op=ALU.mult),
                           reads=["U", ("SZG", zp)], writes=["U"])
                fm_tile(it, ch, pos, ev)
                if g == 1:
                    wrelease(it, 4)

            def gmlp_c(b):
                bm = next_bank()
                for g in range(4):
                    P.emit("pe", lambda e, bm=bm, g=g, b=b, l=l: e.matmul(
                        PSB[bm][:, g * 128:(g + 1) * 128], lhsT=VN[:, b, g * 128:(g + 1) * 128],
                        rhs=WST[:, l, g, :], start=True, stop=True),
                           reads=["VN", "WST"], writes=[("PS", bm)], sig=(g == 3))
                tp = 0
                P.emit("dve", lambda e, bm=bm, tp=tp: e.tensor_tensor(
                    out=T1[tp][:].rearrange("p a b -> p (a b)"), in0=PSB[bm][:], in1=PB1[:, O_BS:O_BS + 512],
                    op=ALU.add),
                       reads=[("PS", bm), "PB1"], writes=[("T1", tp)])
                P.emit("pool", lambda e, tp=tp, b=b: e.tensor_tensor(
                    out=YG[:, :, b * 128:(b + 1) * 128], in0=T1[tp][:], in1=U[:, :, b * 128:(b + 1) * 128],
                    op=ALU.mult),
                       reads=[("T1", tp), "U"], writes=["YG"])
            flush_b(1)
            kb0 = att_scores(0)
            for j in range(4):
                za_tile(j)
            vg_ln(*pend_ln.pop(0))
            flush_b(1)
            kb1 = att_scores(1)
            att_pv(0, kb0)
            for g in range(4):
                u_tile(g)
            vg_ln(*pend_ln.pop(0))
            flush_b(1)
            kb2 = att_scores(2)
            att_pv(1, kb1)
            for g in range(4):
                zg_tile(g)
            wrelease(it, 5)
            flush_b(1)
            kb3 = att_scores(3)
            att_pv(2, kb2)
            for b in range(NB):
                gmlp_c(b)
            att_pv(3, kb3)
            flush_b(99)
            if nxt_l is not None and nxt_l != l:
                load_pb1(nxt_l)
            P.emit("pool", lambda e, l=l, KTC=KTC: e.tensor_copy(out=KTC[:, l, :], in_=KT[:, 512:640]),
                   reads=["KT"], writes=[("KTC", s_)])
            P.emit("pool", lambda e, l=l, VAC=VAC: e.tensor_copy(out=VAC[:, l, :], in_=VA[:, 4, :]),
                   reads=["VA"], writes=[("VAC", s_)])
            P.emit("pool", lambda e, l=l, VBC=VBC: e.tensor_copy(out=VBC[:, l, :], in_=VB[:, 4, :]),
                   reads=["VB"], writes=[("VBC", s_)])
            sba = wslot(it, 6)
            sbg = wslot(it, 7)
            gps = {}

            def gates(c):
                gch = 8 + c // 2
                gp = rotv("sg")
                gps[c] = gp
                fm_tile(it, gch, (c % 2) * 2,
                        lambda bi, gp=gp, c=c: act_evac(bi, lambda: SGA[gp][:], AF.Sigmoid, l, 17 + c, [("SGA", gp)]))
                fm_tile(it, gch, (c % 2) * 2 + 1,
                        lambda bi, gp=gp, c=c: act_evac(bi, lambda: SGG[gp][:], AF.Sigmoid, l, 25 + c, [("SGG", gp)]))
                if c % 2 == 1:
                    wrelease(it, gch)

            early_f = (it + 1 < n_iter) and (order[it + 1][2] != s_)
            if early_f:
                emit_xload(it + 1)
                f_done.add(it + 1)
            gates(0)
            for c in range(8):
                if c + 1 < 8:
                    gates(c + 1)
                if early_f and c == 4:
                    emit_F_cast(it + 1, 0)
                if early_f and c == 5:
                    emit_F_cast(it + 1, 1)
                if early_f and c == 6:
                    emit_F_tr(it + 1, 0)
                    emit_F_cast(it + 1, 2)
                if early_f and c == 7:
                    emit_F_tr(it + 1, 1)
                    emit_F_cast(it + 1, 3)
                gp = gps[c]
                ba = next_bank()
                for kc in range(4):
                    P.emit("pe", lambda e, ba=ba, kc=kc, c=c, sba=sba: e.matmul(
                        PSB[ba][:], lhsT=WR[sba][:, kc * 1024 + c * 128: kc * 1024 + (c + 1) * 128],
                        rhs=YA[:, kc, :], start=(kc == 0), stop=(kc == 3)),
                           reads=[("WR", sba), "YA"], writes=[("PS", ba)], sig=(kc == 3))
                bgk = next_bank()
                for kc in range(4):
                    P.emit("pe", lambda e, bgk=bgk, kc=kc, c=c, sbg=sbg: e.matmul(
                        PSB[bgk][:], lhsT=WR[sbg][:, kc * 1024 + c * 128: kc * 1024 + (c + 1) * 128],
                        rhs=YG[:, kc, :], start=(kc == 0), stop=(kc == 3)),
                           reads=[("WR", sbg), "YG"], writes=[("PS", bgk)], sig=(kc == 3))
                mp = 0
                P.emit("dve", lambda e, ba=ba, mp=mp, gp=gp: e.tensor_tensor(
                    out=M1[mp][:], in0=PSB[ba][:], in1=SGA[gp][:], op=ALU.mult),
                       reads=[("PS", ba), ("SGA", gp)], writes=[("M1", mp)])
                P.emit("dve", lambda e, bgk=bgk, mp=mp, gp=gp: e.tensor_tensor(
                    out=M2[mp][:], in0=PSB[bgk][:], in1=SGG[gp][:], op=ALU.mult),
                       reads=[("PS", bgk), ("SGG", gp)], writes=[("M2", mp)])
                P.emit("pool", lambda e, mp=mp, c=c: e.tensor_tensor(
                    out=MG[:, c, :], in0=M1[mp][:], in1=M2[mp][:], op=ALU.add),
                       reads=[("M1", mp), ("M2", mp)], writes=["MG"])
            wrelease(it, 6)
            wrelease(it, 7)
            so = [wslot(it, 12), wslot(it, 13)]
            if early_f:
                emit_F_tr(it + 1, 2)
                emit_F_tr(it + 1, 3)
            for b in range(NB):
                for h in range(2):
                    bo = next_bank()
                    for kc in range(KC):
                        P.emit("pe", lambda e, bo=bo, kc=kc, b=b, h=h, soh=so[h]: e.matmul(
                            PSB[bo][:], lhsT=MG[:, kc, b * 128:(b + 1) * 128],
                            rhs=WR[soh][:, kc * 512:(kc + 1) * 512], start=(kc == 0), stop=(kc == KC - 1)),
                               reads=["MG", ("WR", so[h])], writes=[("PS", bo)], sig=(kc == KC - 1))
                    rp = rotv("rt")
                    P.emit("dve", lambda e, bo=bo, rp=rp, h=h: e.tensor_tensor(
                        out=RT[rp][:], in0=PSB[bo][:], in1=PB2[:, O_BOUT + h * 512:O_BOUT + (h + 1) * 512],
                        op=ALU.add),
                           reads=[("PS", bo), "PB2"], writes=[("RT", rp)])
                    P.emit("dve", lambda e, rp=rp, b=b, h=h, X=X: e.scalar_tensor_tensor(
                        out=X[:, b, h * 512:(h + 1) * 512], in0=X[:, b, h * 512:(h + 1) * 512], scalar=ALPHA,
                        in1=RT[rp][:], op0=ALU.mult, op1=ALU.add),
                           reads=[("X", s_, b), ("RT", rp)], writes=[("X", s_, b)])
                for h in range(2):
                    P.emit("dve", lambda e, b=b, h=h, X=X: e.bn_stats(out=ST6E[b][:, h * 6:(h + 1) * 6],
                                                                      in_=X[:, b, h * 512:(h + 1) * 512]),
                           reads=[("X", s_, b)], writes=[("ST6E", b)])

                def tail(b=b, X=X, s_=s_, tok0=tok0, last_layer=last_layer):
                    par = rotv("stat")
                    P.emit("dve", lambda e: e.bn_aggr(out=MV[par][:], in_=ST6E[b][:, 0:12]),
                           reads=[("ST6E", b)], writes=[("MV", par)])
                    P.emit("pool", lambda e: e.tensor_scalar(out=SD[par][:], in0=MV[par][:, 1:2], scalar1=LN_EPS,
                                                             scalar2=None, op0=ALU.add),
                           reads=[("MV", par)], writes=[("SD", par)])
                    P.emit("pool", lambda e: e.tensor_tensor(out=RS[par][:], in0=SD[par][:], in1=NEGH[:], op=ALU.pow),
                           reads=[("SD", par), "NEGH"], writes=[("RS", par)])
                    P.emit("dve", lambda e: e.scalar_tensor_tensor(out=NM[par][:], in0=MV[par][:, 0:1], scalar=-1.0,
                                                                   in1=RS[par][:], op0=ALU.mult, op1=ALU.mult),
                           reads=[("MV", par), ("RS", par)], writes=[("NM", par)])
                    P.emit("act", lambda e: e.activation(
                        out=X[:, b, :], in_=X[:, b, :], func=AF.Identity, scale=RS[par][:], bias=NM[par][:]),
                           reads=[("X", s_, b), ("RS", par), ("NM", par)], writes=[("X", s_, b)])

                def tail_b(b=b, X=X, s_=s_, tok0=tok0, last_layer=last_layer):
                    P.emit("pool", lambda e: e.tensor_tensor(
                        out=X[:, b, :], in0=X[:, b, :], in1=PB2[:, O_LNG:O_LNG + 1024], op=ALU.mult),
                           reads=[("X", s_, b), "PB2"], writes=[("X", s_, b)])
                    P.emit("pool", lambda e: e.tensor_tensor(
                        out=X[:, b, :], in0=X[:, b, :], in1=PB2[:, O_LNB:O_LNB + 1024], op=ALU.add),
                           reads=[("X", s_, b), "PB2"], writes=[("X", s_, b)])
                    if last_layer:
                        op = P.emit("sp", lambda e: e.dma_start(
                            out=out_d[tok0 + b * 128: tok0 + (b + 1) * 128, :], in_=X[:, b, :]),
                                    reads=[("X", s_, b)], dma=f"o{s_}_{b}")
                        store_toks.append(op.tok)
                deferred.append(tail)
                deferred_b.append(tail_b)
            wrelease(it, 12)
            wrelease(it, 13)
            if nxt_l is not None and nxt_l != l:
                deferred_b.append(lambda nxt_l=nxt_l: load_pb2(nxt_l))
        flush_deferred()
        for tok in store_toks:
            P.wait_tok("sp", tok)

        with nc.Block() as block:
            @block.tensor
            def _(e):
                for f in P.streams["pe"]:
                    f(e)

            @block.scalar
            def _(e):
                for f in P.streams["act"]:
                    f(e)

            @block.vector
            def _(e):
                for f in P.streams["dve"]:
                    f(e)

            @block.gpsimd
            def _(e):
                for f in P.streams["pool"]:
                    f(e)

            @block.sync
            def _(e):
                for f in P.streams["sp"]:
                    f(e)
    return nc


def _tile_cols():
    def q_tile(base, j):
        return list(range(base + j * 64, base + (j + 1) * 64)) + list(range(base + (4 + j) * 64, base + (5 + j) * 64))

    def nat(base, t):
        return list(range(base + t * 128, base + (t + 1) * 128))
    Q0, K0, V0, ZA0, U0, VG0, ZG0, GA0, GG0 = 0, 512, 640, 768, 1280, 1792, 2304, 2816, 3840
    tiles = []
    tiles += [q_tile(Q0, j) for j in range(4)]
    tiles += [nat(K0, 0), nat(V0, 0), q_tile(ZA0, 0), q_tile(ZA0, 1)]
    tiles += [nat(VG0, t) for t in range(4)]
    tiles += [q_tile(ZA0, 2), q_tile(ZA0, 3), nat(U0, 0), nat(U0, 1)]
    tiles += [nat(U0, 2), nat(U0, 3), nat(ZG0, 0), nat(ZG0, 1)]
    tiles += [nat(ZG0, 2), nat(ZG0, 3), None, None]
    return tiles


def _gate_tiles():
    GA0, GG0 = 2816, 3840
    tiles = []
    for c in range(8):
        tiles.append(list(range(GA0 + c * 128, GA0 + (c + 1) * 128)))
        tiles.append(list(range(GG0 + c * 128, GG0 + (c + 1) * 128)))
    return tiles


def pack_host(inputs, depth):
    w_in = np.asarray(inputs["w_in"], np.float32)
    b_in = np.asarray(inputs["b_in"], np.float32)
    sinks = np.asarray(inputs["attn_sinks"], np.float32)
    w_ba = np.asarray(inputs["w_branch_attn"], np.float32)
    w_bg = np.asarray(inputs["w_branch_gmlp"], np.float32)
    w_out = np.asarray(inputs["w_out"], np.float32)
    wpk = np.zeros((depth, NCHUNK, 128, CH), np.float32)
    pcol = np.zeros((128, depth, PCW), np.float32)
    prow = np.zeros((depth, PROW), np.float32)
    tiles = _tile_cols()
    gtiles = _gate_tiles()
    attn_rows = []
    for j in range(4):
        attn_rows += list(range(j * 64, (j + 1) * 64)) + list(range((4 + j) * 64, (5 + j) * 64))
    attn_rows = np.array(attn_rows)
    fm_bias_tiles = ([tiles[j] for j in range(4)] + [tiles[4]] + [tiles[6], tiles[7], tiles[12], tiles[13]] +
                     [tiles[14], tiles[15], tiles[16], tiles[17]] + [tiles[18], tiles[19], tiles[20], tiles[21]] +
                     [gtiles[2 * c] for c in range(8)] + [gtiles[2 * c + 1] for c in range(8)])
    for l in range(depth):
        wl = w_in[l].reshape(KC, 128, -1)
        for ch in range(6):
            blk = wpk[l, ch].reshape(128, KC, 512)
            for pos in range(4):
                cols = tiles[ch * 4 + pos]
                if cols is None:
                    continue
                blk[:, :, pos * 128:(pos + 1) * 128] = wl[:, :, cols].transpose(1, 0, 2)
        for ch in range(4):
            blk = wpk[l, 8 + ch].reshape(128, KC, 512)
            for pos in range(4):
                cols = gtiles[ch * 4 + pos]
                blk[:, :, pos * 128:(pos + 1) * 128] = wl[:, :, cols].transpose(1, 0, 2)
        wpk[l, 6].reshape(128, 4, 1024)[:] = w_ba[l][attn_rows].reshape(4, 128, 1024).transpose(1, 0, 2)
        wpk[l, 7].reshape(128, 4, 1024)[:] = w_bg[l].reshape(4, 128, 1024).transpose(1, 0, 2)
        wo = w_out[l].reshape(KC, 128, 1024)
        wpk[l, 12].reshape(128, KC, 512)[:] = wo[:, :, 0:512].transpose(1, 0, 2)
        wpk[l, 13].reshape(128, KC, 512)[:] = wo[:, :, 512:1024].transpose(1, 0, 2)
        for i, cols in enumerate(fm_bias_tiles):
            pcol[:, l, i] = b_in[l][cols]
        for j in range(4):
            pcol[0:64, l, 33 + j] = sinks[l, j]
            pcol[64:128, l, 33 + j] = sinks[l, 4 + j]
        prow[l, O_BV:O_BV + 128] = b_in[l][640:768]
        prow[l, O_BVG:O_BVG + 512] = b_in[l][1792:2304]
        prow[l, O_GLG:O_GLG + 512] = inputs["gmlp_ln_g"][l]
        prow[l, O_GLB:O_GLB + 512] = inputs["gmlp_ln_b"][l]
        prow[l, O_BS:O_BS + 512] = np.asarray(inputs["b_spatial"][l], np.float32).reshape(512)
        prow[l, PB1W + O_BOUT:PB1W + O_BOUT + 1024] = inputs["b_out"][l]
        prow[l, PB1W + O_LNG:PB1W + O_LNG + 1024] = inputs["ln_g"][l]
        prow[l, PB1W + O_LNB:PB1W + O_LNB + 1024] = inputs["ln_b"][l]
    wsp = np.ascontiguousarray(np.asarray(inputs["w_spatial"], np.float32)[:depth].transpose(0, 3, 1, 2)).reshape(depth * 128, 512)
    return (wpk.reshape(depth * NCHUNK * 128, CH), np.ascontiguousarray(prow),
            np.ascontiguousarray(pcol.reshape(128, depth * PCW)), wsp)


def const_tables():
    cst = np.zeros((128, CSTW), np.float32)
    k = np.arange(128)[:, None]
    q = np.arange(128)[None, :]
    bias = np.zeros((128, 2, 8, 128), np.float32)
    for h in range(8):
        slope = 2.0 ** (-(h + 1))
        dist0 = (q - k + 128).astype(np.float32)
        bias[:, 0, h, :] = np.where(k > q, -8.0 * slope * dist0, -1.0e6)
        dist1 = (q - k).astype(np.float32)
        bias[:, 1, h, :] = np.where(k <= q, -8.0 * slope * dist1, -1.0e6)
    cst[:, 0:2048] = bias.reshape(128, 2048)
    cst[:, 2048:2176] = (k <= q).astype(np.float32)
    cst[:, 2176:2304] = np.eye(128, dtype=np.float32)
    return cst


_NC_CACHE = {}


def run(inputs, n_cores, depth):
    x = np.asarray(inputs["x"], np.float32)
    B, S, _ = x.shape
    assert B % n_cores == 0
    nseq = B // n_cores
    key = (nseq, S, depth)
    if key not in _NC_CACHE:
        _NC_CACHE[key] = build_program(nseq, S, depth)
    nc = _NC_CACHE[key]
    wpk, prow, pcol, wsp = pack_host(inputs, depth)
    cst = const_tables()
    xs = x.reshape(n_cores, nseq * S, D)
    in_maps = [{"x": xs[c], "wpk": wpk, "prow": prow, "pcol": pcol, "wsp": wsp, "cst": cst} for c in range(n_cores)]
    res = run_bass_kernel_spmd(nc, in_maps, core_ids=list(range(n_cores)))
    out = np.stack([res.results[c]["out"] for c in range(n_cores)], axis=0)
    return out.reshape(B, S, D).astype(np.float32)


def kernel(x, w_in, b_in, attn_sinks, gmlp_ln_g, gmlp_ln_b, w_spatial, b_spatial,
           w_branch_attn, w_branch_gmlp, w_out, b_out, ln_g, ln_b):
    inputs = dict(x=x, w_in=w_in, b_in=b_in, attn_sinks=attn_sinks, gmlp_ln_g=gmlp_ln_g, gmlp_ln_b=gmlp_ln_b,
                  w_spatial=w_spatial, b_spatial=b_spatial, w_branch_attn=w_branch_attn,
                  w_branch_gmlp=w_branch_gmlp, w_out=w_out, b_out=b_out, ln_g=ln_g, ln_b=ln_b)
    inputs = {k: np.asarray(v) for k, v in inputs.items()}
    return run(inputs, 8, 4)
```
